# Optimizing a Trainium2 kernel written in Bass

```python
import math
import jax, jax.numpy as jnp
from jax import lax
import numpy as np

D_MODEL = 1024
BATCH = 4
SEQ = 8192
DEPTH = 1

CHUNK = 64
D_MIX = D_MODEL
ATTN_HEADS = 8
ATTN_KV_HEADS = 2
HEAD_DIM = 64
ATTN_GROUP = ATTN_HEADS // ATTN_KV_HEADS
WINDOW = 128
WIN_CHUNKS = WINDOW // CHUNK
BAND = (WIN_CHUNKS + 1) * CHUNK
D_ATTN = ATTN_HEADS * HEAD_DIM
D_KV = ATTN_KV_HEADS * HEAD_DIM
POOL_WINDOWS = (2, 4, 8, 16)
POOL_GROUPS = len(POOL_WINDOWS)
D_POOL = D_MIX - D_ATTN
POOL_GROUP_DIM = D_POOL // POOL_GROUPS
D_IN = D_ATTN + 2 * D_KV + D_POOL
REL_BUCKETS = 32
REL_MAX_DIST = 128
PEER_KEYS = 128
PEER_EXPERTS = PEER_KEYS * PEER_KEYS
PEER_HEADS = 8
PEER_TOPK = 16
PEER_QDIM = 256
PEER_HALF = PEER_QDIM // 2
PEER_TOKEN_BLOCK = 128
EPS = 1e-6
NEG_INF = -1e30

kernel_name = "hybrid_swa_pool_peer_block"


def rms_norm(x, g):
    xf = x.astype(jnp.float32)
    y = xf * lax.rsqrt(jnp.mean(xf * xf, axis=-1, keepdims=True) + EPS)
    return (y * g.astype(jnp.float32)).astype(x.dtype)


def t5_bucket(rel):
    nb = REL_BUCKETS // 2
    max_exact = nb // 2
    base = jnp.where(rel > 0, nb, 0)
    n = jnp.abs(rel)
    nf = jnp.maximum(n, 1).astype(jnp.float32)
    large = max_exact + (jnp.log(nf / max_exact) / math.log(REL_MAX_DIST / max_exact)
                         * (nb - max_exact)).astype(jnp.int32)
    large = jnp.minimum(large, nb - 1)
    return base + jnp.where(n < max_exact, n, large)


def band_bias(rel_bias):
    i = jnp.arange(CHUNK)[:, None]
    j = jnp.arange(BAND)[None, :]
    rel = (j - WIN_CHUNKS * CHUNK) - i
    b = jnp.take(rel_bias, t5_bucket(rel), axis=0)
    b = jnp.transpose(b, (2, 0, 1)).astype(jnp.float32)
    return b.reshape(ATTN_KV_HEADS, ATTN_GROUP, CHUNK, BAND)


def to_band(t, n_chunks):
    pad = [(0, 0), (WIN_CHUNKS, 0)] + [(0, 0)] * (t.ndim - 2)
    tp = jnp.pad(t, pad)
    return jnp.concatenate([tp[:, i:i + n_chunks] for i in range(WIN_CHUNKS + 1)], axis=2)


def swa_attention(q, k, v, q_norm_g, k_norm_g, sinks, rel_bias):
    B, S, _ = q.shape
    nc = S // CHUNK
    q = rms_norm(q.reshape(B, S, ATTN_HEADS, HEAD_DIM), q_norm_g)
    k = rms_norm(k.reshape(B, S, ATTN_KV_HEADS, HEAD_DIM), k_norm_g)
    v = v.reshape(B, S, ATTN_KV_HEADS, HEAD_DIM)
    qc = q.reshape(B, nc, CHUNK, ATTN_KV_HEADS, ATTN_GROUP, HEAD_DIM)
    kb = to_band(k.reshape(B, nc, CHUNK, ATTN_KV_HEADS, HEAD_DIM), nc)
    vb = to_band(v.reshape(B, nc, CHUNK, ATTN_KV_HEADS, HEAD_DIM), nc)
    s = jnp.einsum('bcqhgd,bckhd->bchgqk', qc, kb,
                   preferred_element_type=jnp.float32) * (HEAD_DIM ** -0.5)
    s = s + band_bias(rel_bias)[None, None]
    key_chunk = jnp.arange(nc)[:, None] - WIN_CHUNKS + jnp.arange(BAND)[None, :] // CHUNK
    valid = (key_chunk >= 0)[None, :, None, None, None, :]
    s = jnp.where(valid, s, NEG_INF)
    sink = sinks.astype(jnp.float32).reshape(1, 1, ATTN_KV_HEADS, ATTN_GROUP, 1, 1)
    m = jnp.maximum(jnp.max(s, axis=-1, keepdims=True), sink)
    p = jnp.exp(s - m)
    p = p / (jnp.sum(p, axis=-1, keepdims=True) + jnp.exp(sink - m))
    o = jnp.einsum('bchgqk,bckhd->bcqhgd', p.astype(vb.dtype), vb)
    return o.reshape(B, S, D_ATTN)


def causal_mean(xg, w):
    S = xg.shape[1]
    cs = jnp.pad(jnp.cumsum(xg, axis=1), ((0, 0), (1, 0), (0, 0)))
    t = jnp.arange(S)
    lo = jnp.maximum(t + 1 - w, 0)
    total = cs[:, 1:] - jnp.take(cs, lo, axis=1)
    cnt = jnp.minimum(t + 1, w).astype(jnp.float32)
    return total / cnt[None, :, None]


def pool_mixer(p, pool_w, pool_scale):
    B, S, _ = p.shape
    pf = p.astype(jnp.float32).reshape(B, S, POOL_GROUPS, POOL_GROUP_DIM)
    pooled = jnp.stack([causal_mean(pf[:, :, g], w) for g, w in enumerate(POOL_WINDOWS)], axis=2)
    d = (pooled - pf).astype(p.dtype)
    o = jnp.einsum('bsgc,gcd->bsgd', d, pool_w)
    o = o * pool_scale.reshape(POOL_GROUPS, POOL_GROUP_DIM)
    return o.reshape(B, S, D_POOL)


def peer(h, wq, subkeys, u_tab, v_tab):
    B, S, D = h.shape
    T = B * S
    ht = h.reshape(T, D)
    q = (ht @ wq).reshape(T, PEER_HEADS, 2, PEER_HALF)
    sc = jnp.einsum('thpd,hpnd->thpn', q, subkeys, preferred_element_type=jnp.float32)
    v1, i1 = lax.top_k(sc[:, :, 0], PEER_TOPK)
    v2, i2 = lax.top_k(sc[:, :, 1], PEER_TOPK)
    cand = (v1[..., :, None] + v2[..., None, :]).reshape(T, PEER_HEADS, PEER_TOPK * PEER_TOPK)
    best, pos = lax.top_k(cand, PEER_TOPK)
    e1 = jnp.take_along_axis(i1, pos // PEER_TOPK, axis=-1)
    e2 = jnp.take_along_axis(i2, pos % PEER_TOPK, axis=-1)
    experts = (e1 * PEER_KEYS + e2).reshape(T, PEER_HEADS * PEER_TOPK)
    gates = jax.nn.softmax(best, axis=-1).reshape(T, PEER_HEADS * PEER_TOPK).astype(h.dtype)

    def block(args):
        xb, eb, gb = args
        u = jnp.take(u_tab, eb, axis=0)
        vv = jnp.take(v_tab, eb, axis=0)
        act = jax.nn.gelu(jnp.einsum('tkd,td->tk', u, xb), approximate=False)
        return jnp.einsum('tk,tkd->td', gb * act, vv)

    nb = T // PEER_TOKEN_BLOCK
    out = lax.map(block, (ht.reshape(nb, PEER_TOKEN_BLOCK, D),
                          experts.reshape(nb, PEER_TOKEN_BLOCK, -1),
                          gates.reshape(nb, PEER_TOKEN_BLOCK, -1)))
    return out.reshape(B, S, D)


def setup_inputs(seed: int = 0) -> dict:
    key = jax.random.key(seed)
    ks = jax.random.split(key, 15)
    n = jax.random.normal
    L = DEPTH
    return {
        "x": n(ks[0], (BATCH, SEQ, D_MODEL), jnp.float32),
        "norm1_g": 1.0 + 0.02 * n(ks[1], (L, D_MODEL), jnp.float32),
        "w_in": n(ks[2], (L, D_MODEL, D_IN), jnp.float32) * D_MODEL ** -0.5,
        "q_norm_g": 1.0 + 0.02 * n(ks[3], (L, HEAD_DIM), jnp.float32),
        "k_norm_g": 1.0 + 0.02 * n(ks[4], (L, HEAD_DIM), jnp.float32),
        "attn_sinks": 0.5 * n(ks[5], (L, ATTN_HEADS), jnp.float32),
        "rel_bias": 0.5 * n(ks[6], (REL_BUCKETS, ATTN_HEADS), jnp.float32),
        "pool_w": n(ks[7], (L, POOL_GROUPS, POOL_GROUP_DIM, POOL_GROUP_DIM), jnp.float32) * POOL_GROUP_DIM ** -0.5,
        "pool_scale": 1.0 + 0.02 * n(ks[8], (L, D_POOL), jnp.float32),
        "w_out": n(ks[9], (L, D_MIX, D_MODEL), jnp.float32) * D_MIX ** -0.5,
        "norm2_g": 1.0 + 0.02 * n(ks[10], (L, D_MODEL), jnp.float32),
        "peer_wq": n(ks[11], (L, D_MODEL, PEER_HEADS * PEER_QDIM), jnp.float32) * D_MODEL ** -0.5,
        "peer_subkeys": n(ks[12], (L, PEER_HEADS, 2, PEER_KEYS, PEER_HALF), jnp.float32) * PEER_HALF ** -0.5,
        "peer_u": n(ks[13], (L, PEER_EXPERTS, D_MODEL), jnp.float32) * D_MODEL ** -0.5,
        "peer_v": n(ks[14], (L, PEER_EXPERTS, D_MODEL), jnp.float32) * (PEER_HEADS * PEER_TOPK) ** -0.5,
    }


def reference(x, norm1_g, w_in, q_norm_g, k_norm_g, attn_sinks, rel_bias, pool_w, pool_scale,
              w_out, norm2_g, peer_wq, peer_subkeys, peer_u, peer_v):
    for l in range(DEPTH):
        h = rms_norm(x, norm1_g[l])
        y = h @ w_in[l]
        q = y[..., :D_ATTN]
        k = y[..., D_ATTN:D_ATTN + D_KV]
        v = y[..., D_ATTN + D_KV:D_ATTN + 2 * D_KV]
        p = y[..., D_ATTN + 2 * D_KV:]
        a = swa_attention(q, k, v, q_norm_g[l], k_norm_g[l], attn_sinks[l], rel_bias)
        b = pool_mixer(p, pool_w[l], pool_scale[l])
        x = x + jnp.concatenate([a, b], axis=-1) @ w_out[l]
        h2 = rms_norm(x, norm2_g[l])
        x = x + peer(h2, peer_wq[l], peer_subkeys[l], peer_u[l], peer_v[l])
    return x
```

```python
import numpy as np
import ml_dtypes
from contextlib import ExitStack
import concourse.bass as bass
import concourse.mybir as mybir
from concourse.bass_utils import run_bass_kernel_spmd

F32 = mybir.dt.float32
BF16 = mybir.dt.bfloat16
U32 = mybir.dt.uint32
AF = mybir.ActivationFunctionType
ALU = mybir.AluOpType
AX = mybir.AxisListType

NCORES = 8
TOK = 4096
NBLK = 32
D = 1024
NEG = -30000.0
EPS = 1e-6
TT = 256
NTILE = TOK // TT
CH = 16
COMPUTE = ("pe", "act", "dve", "pool")


class Prog:
    def __init__(self, nc):
        self.nc = nc
        self.eng = {"pe": nc.tensor, "act": nc.scalar, "dve": nc.vector,
                    "pool": nc.gpsimd, "sp": nc.sync}
        self.ops = []
        self.res = {}
        self.sems = {}
        self.sig_total = {e: 0 for e in COMPUTE}
        self.dma_total = {}
        self.dma_waited = {e: {} for e in self.eng}
        self.nops_total = 0
        self.alias = {}
        self.capture = None
        self.fin = []
        self.efree = {}

    def _sem(self, key):
        if key not in self.sems:
            nm = "s_" + "_".join(str(k) for k in key)
            self.sems[key] = self.nc.alloc_semaphore(name=nm)
        return self.sems[key]

    def _resolve(self, reads, writes, pwrites):
        al = self.alias
        return (tuple(al.get(k, k) for k in reads), tuple(al.get(k, k) for k in writes),
                tuple(al.get(k, k) for k in pwrites))

    def _deps(self, reads, writes, pwrites):
        deps = set()
        for k in reads:
            st = self.res.get(k)
            if st is not None:
                deps.update(st["w"])
        for k, partial in [(k, False) for k in writes] + [(k, True) for k in pwrites]:
            st = self.res.get(k)
            if st is None:
                continue
            if st["r"] or k in reads:
                deps.update(st["r"])
                deps.update(st["w"])
            elif partial:
                deps.update(st["g"])
            else:
                deps.update(st["w"])
                deps.update(st["g"])
        return deps

    def _add(self, kind, eng, emit, reads, writes, pwrites, semkey=None, ndma=1, cost=0):
        idx = len(self.ops)
        reads, writes, pwrites = self._resolve(reads, writes, pwrites)
        deps = set()
        for k in reads:
            st = self.res.get(k)
            if st is None:
                st = self.res[k] = {"w": [], "r": [], "g": []}
            deps.update(st["w"])
            st["r"].append(idx)
        for k, partial in [(k, False) for k in writes] + [(k, True) for k in pwrites]:
            st = self.res.get(k)
            if st is None:
                st = self.res[k] = {"w": [], "r": [], "g": []}
            if st["r"]:
                g = list(st["r"]) + list(st["w"])
                deps.update(g)
                st["g"] = g
                st["w"] = [idx]
                st["r"] = []
            elif partial:
                deps.update(st["g"])
                st["w"].append(idx)
            else:
                deps.update(st["w"])
                deps.update(st["g"])
                st["w"] = [idx]
        deps.discard(idx)
        ready = 0.0
        for d in deps:
            f = self.fin[d]
            if self.ops[d]["eng"] != eng or kind == "d" or self.ops[d]["kind"] == "d":
                f += 300.0
            elif eng != "pe":
                f += 170.0
            ready = max(ready, f)
        start = max(ready, self.efree.get(eng, 0.0))
        if kind == "c":
            self.efree[eng] = start + cost
            self.fin.append(start + cost)
        else:
            self.efree[eng] = start + (1500.0 if eng == "pool" else 120.0)
            self.fin.append(start + 2200.0 + cost)
        self.ops.append(dict(kind=kind, eng=eng, emit=emit, deps=deps, semkey=semkey,
                             ndma=ndma, signal=False, waits=[]))
        return idx

    def c(self, eng, emit, reads=(), writes=(), pwrites=(), cost=None):
        if cost is None:
            cost = {"pe": 120.0, "act": 500.0, "dve": 350.0, "pool": 1200.0}.get(eng, 100.0)
        a = ("c", eng, emit, tuple(reads), tuple(writes), tuple(pwrites), None, 1, float(cost))
        if self.capture is not None:
            self.capture.append(a)
            return None
        return self._add(*a)

    def dma(self, eng, emit, semkey, reads=(), writes=(), pwrites=(), ndma=1, cost=1000.0):
        a = ("d", eng, emit, tuple(reads), tuple(writes), tuple(pwrites), semkey, ndma, float(cost))
        if self.capture is not None:
            self.capture.append(a)
            return None
        return self._add(*a)

    def schedule(self, *lists):
        lists = [l for l in lists if l]
        idx = [0] * len(lists)
        while True:
            best, bt = None, None
            for i, l in enumerate(lists):
                if idx[i] >= len(l):
                    continue
                kind, eng, _, reads, writes, pwrites, _, _, cost = l[idx[i]]
                r, w, pw = self._resolve(reads, writes, pwrites)
                ready = 0.0
                for d in self._deps(r, w, pw):
                    f = self.fin[d] + (300.0 if (self.ops[d]["eng"] != eng or kind == "d"
                                                 or self.ops[d]["kind"] == "d") else 170.0)
                    ready = max(ready, f)
                t = max(ready, self.efree.get(eng, 0.0))
                if bt is None or t < bt - 1e-9:
                    best, bt = i, t
            if best is None:
                break
            self._add(*lists[best][idx[best]])
            idx[best] += 1

    def merge(self, *lists):
        lists = [l for l in lists if l]
        idx = [0] * len(lists)
        while True:
            best, bf = None, None
            for i, l in enumerate(lists):
                if idx[i] < len(l):
                    f = idx[i] / len(l)
                    if bf is None or f < bf:
                        best, bf = i, f
            if best is None:
                break
            self._add(*lists[best][idx[best]])
            idx[best] += 1

    def flush(self):
        ops = self.ops
        pos = {e: 0 for e in COMPUTE}
        by_pos = {}
        last = {}
        for i, op in enumerate(ops):
            if op["kind"] == "c":
                pos[op["eng"]] += 1
                op["pos"] = pos[op["eng"]]
                by_pos[(op["eng"], op["pos"])] = i
                last[op["eng"]] = i
        wpos = {e: {E: 0 for E in COMPUTE} for e in self.eng}
        dma_val = {}
        for i, op in enumerate(ops):
            e = op["eng"]
            need_c = {}
            need_d = {}
            for d in op["deps"]:
                p = ops[d]
                if p["kind"] == "c":
                    if p["eng"] == "pe" and e == "pe" and op["kind"] == "c":
                        continue
                    need_c[p["eng"]] = max(need_c.get(p["eng"], 0), p["pos"])
                else:
                    need_d[p["semkey"]] = max(need_d.get(p["semkey"], 0), dma_val[d])
            for E, v in need_c.items():
                if wpos[e][E] >= v:
                    continue
                wpos[e][E] = v
                op["waits"].append(("c", E, v))
                ops[by_pos[(E, v)]]["signal"] = True
            for k, v in need_d.items():
                if self.dma_waited[e].get(k, 0) >= v:
                    continue
                self.dma_waited[e][k] = v
                op["waits"].append(("d", k, v))
            if op["kind"] == "d":
                k = op["semkey"]
                self.dma_total[k] = self.dma_total.get(k, 0) + 16 * op["ndma"]
                dma_val[i] = self.dma_total[k]
        for E, i in last.items():
            ops[i]["signal"] = True
        cnt = dict(self.sig_total)
        sigcount = {}
        for i, op in enumerate(ops):
            if op["kind"] == "c":
                if op["signal"]:
                    cnt[op["eng"]] += 1
                sigcount[(op["eng"], op["pos"])] = cnt[op["eng"]]
        for op in ops:
            E = self.eng[op["eng"]]
            for kind, k, v in op["waits"]:
                if kind == "c":
                    E.wait_ge(self._sem(("c", k)), sigcount[(k, v)])
                else:
                    E.wait_ge(self._sem(("d", k)), v)
            r = op["emit"](E)
            if op["kind"] == "c":
                if op["signal"]:
                    r.then_inc(self._sem(("c", op["eng"])), 1)
            else:
                rs = r if isinstance(r, (list, tuple)) else [r]
                assert len(rs) == op["ndma"]
                for x in rs:
                    x.then_inc(self._sem(("d", op["semkey"])), 16)
        self.sig_total = cnt
        for e, E in self.eng.items():
            for Ec in COMPUTE:
                if cnt[Ec] > 0 and Ec != e:
                    E.wait_ge(self._sem(("c", Ec)), cnt[Ec])
            for k, v in self.dma_total.items():
                if self.dma_waited[e].get(k, 0) < v:
                    E.wait_ge(self._sem(("d", k)), v)
                    self.dma_waited[e][k] = v
        self.nops_total += len(ops)
        tb = max(list(self.efree.values()) + self.fin + [0.0])
        self.efree = {e: tb for e in self.eng}
        self.fin = []
        self.ops = []
        self.res = {}


def build(debug=False, stop_after=None, only_a=False, nblk_run=NBLK, a_stage=99):
    nc = bass.Bass("TRN2", target_bir_lowering=False)

    def din(name, shape, dt=F32):
        return nc.dram_tensor(name, list(shape), dt, kind="ExternalInput")

    xh = din("xh", [TOK + 128, D])
    biasT = din("biasT", [128, 3, 8, 128])
    pool_inv = din("pool_inv", [128, 4, 16])
    cident = din("cident", [128, 128])
    ciota = din("ciota", [128, 128])
    cblk = din("cblk", [128, 128])
    ci16 = din("ci16", [128, 2, 16])
    w_in_l = din("w_in_l", [128, 8, 1408])
    woa_l = din("woa_l", [64, 8, 1024])
    wob_l = din("wob_l", [128, 4, 1024])
    wq_l = din("wq_l", [128, 8, 2048])
    skT_l = din("skT_l", [128, 16, 128])
    poolw_l = din("poolw_l", [128, 4, 128])
    pscale_l = din("pscale_l", [128, 4])
    g1_l = din("g1_l", [128, 8])
    g2_l = din("g2_l", [128, 8])
    qg_l = din("qg_l", [128, 1])
    kg_l = din("kg_l", [128, 1])
    sink_l = din("sink_l", [64, 8])
    if not only_a:
        peer_u = din("peer_u", [16384, D])
        peer_v = din("peer_v", [16384, D])
    out = nc.dram_tensor("out", [TOK, D], F32, kind="ExternalOutput")

    UTs = nc.dram_tensor("UTs", [128, 128, 1024], BF16, kind="Internal")
    Vs = nc.dram_tensor("Vs", [16384, D], BF16, kind="Internal")
    skind = "ExternalOutput" if debug else "Internal"
    x1s = nc.dram_tensor("x1s", [TOK, D], F32, kind=skind)
    h2s = nc.dram_tensor("h2s", [NTILE, 128, 8, TT], BF16, kind="Internal")
    Rs = nc.dram_tensor("Rs", [NTILE, 128, 3, TT], BF16, kind=skind)

    P = Prog(nc)

    def fs(ap):
        n = 1
        for d_ in ap.shape[1:]:
            n *= int(d_)
        return n

    def ecost(eng, n):
        if eng == "dve":
            return (150.0 + n) / 0.96
        if eng == "act":
            return (224.0 + n) / 1.4
        if eng == "pool":
            return 450.0 + 1.5 * n
        return 120.0

    def DMA(q, out_, in_, semkey, reads=(), writes=(), pwrites=()):
        P.dma(q, lambda e: e.dma_start(out=out_, in_=in_), semkey, reads, writes, pwrites,
              cost=fs(out_) * 0.5)

    def ACTF(out_, in_, func, reads=(), writes=(), pwrites=(), **kw):
        P.c("act", lambda e: e.activation(out=out_, in_=in_, func=func, **kw), reads, writes, pwrites,
            cost=ecost("act", fs(out_)))

    def COPY(eng, out_, in_, reads=(), writes=(), pwrites=()):
        if eng == "act":
            P.c("act", lambda e: e.copy(out=out_, in_=in_), reads, writes, pwrites, cost=ecost("act", fs(out_)))
        else:
            P.c(eng, lambda e: e.tensor_copy(out=out_, in_=in_), reads, writes, pwrites,
                cost=ecost(eng, fs(out_)))

    def TTOP(eng, out_, in0, in1, op, reads=(), writes=(), pwrites=()):
        P.c(eng, lambda e: e.tensor_tensor(out=out_, in0=in0, in1=in1, op=op), reads, writes, pwrites,
            cost=ecost(eng, fs(out_)))

    def TS(eng, out_, in0, s1, s2, op0, op1, reads=(), writes=(), pwrites=()):
        if op1 is None:
            P.c(eng, lambda e: e.tensor_scalar(out=out_, in0=in0, scalar1=s1, scalar2=None, op0=op0),
                reads, writes, pwrites, cost=ecost(eng, fs(out_)))
        else:
            P.c(eng, lambda e: e.tensor_scalar(out=out_, in0=in0, scalar1=s1, scalar2=s2, op0=op0, op1=op1),
                reads, writes, pwrites, cost=ecost(eng, fs(out_)))

    def STT(eng, out_, in0, scalar, in1, op0, op1, reads=(), writes=(), pwrites=()):
        P.c(eng, lambda e: e.scalar_tensor_tensor(out=out_, in0=in0, scalar=scalar, in1=in1, op0=op0, op1=op1),
            reads, writes, pwrites, cost=ecost(eng, fs(out_)))

    PE_GHZ = [1.2]

    def MM(out_, lhsT, rhs, start, stop, reads, writes=(), pwrites=()):
        P.c("pe", lambda e: e.matmul(out_, lhsT, rhs, start=start, stop=stop), reads, writes, pwrites,
            cost=max(110.0 * 1.2 / PE_GHZ[0], fs(out_) / PE_GHZ[0]))

    def TR(out_, in_, ident, reads, writes=(), pwrites=()):
        P.c("pe", lambda e: e.transpose(out=out_, in_=in_, identity=ident), reads, writes, pwrites, cost=110.0)

    def RECIP(out_, in_, reads, writes):
        P.c("dve", lambda e: e.reciprocal(out=out_, in_=in_), reads, writes, cost=(150.0 + 6.5 * fs(out_)) / 0.96)

    with ExitStack() as top:
        def sb(es, name, shape, dt):
            return es.enter_context(nc.sbuf_tensor(name, list(shape), dt))

        def ps(es, name, shape, dt=F32):
            return es.enter_context(nc.psum_tensor(name, list(shape), dt))

        identf = sb(top, "identf", [128, 128], F32)
        identb = sb(top, "identb", [128, 128], BF16)
        iotab = sb(top, "iotab", [128, 128], BF16)
        g2_sb = sb(top, "g2_sb", [128, 8], F32)

        for k_ in ("identf", "identb", "iotab", "g2_sb"):
            P.alias[k_] = "C"
        DMA("sp", identf[:], cident[:], "c", pwrites=["C"])
        DMA("pool", identb[:], cident[:], "c", pwrites=["C"])
        DMA("pool", iotab[:], ciota[:], "c", pwrites=["C"])
        DMA("sp", g2_sb[:], g2_l[:], "c", pwrites=["C"])

        with ExitStack() as es:
            w_in_sb = sb(es, "w_in_sb", [128, 8, 1408], BF16)
            woa_sb = sb(es, "woa_sb", [64, 8, 1024], BF16)
            wob_sb = sb(es, "wob_sb", [128, 4, 1024], BF16)
            wq_sb = sb(es, "wq_sb", [128, 8, 2048], BF16)
            skT_sb = sb(es, "skT_sb", [128, 16, 128], BF16)
            poolw_sb = sb(es, "poolw_sb", [128, 4, 128], BF16)
            bias_sb = sb(es, "bias_sb", [128, 3, 8, 128], F32)
            pinv_sb = sb(es, "pinv_sb", [128, 4, 16], F32)
            cblk_sb = sb(es, "cblk_sb", [128, 128], BF16)
            i16_sb = sb(es, "i16_sb", [128, 2, 16], F32)
            pscale_sb = sb(es, "pscale_sb", [128, 4], F32)
            g1_sb = sb(es, "g1_sb", [128, 8], F32)
            qg_sb = sb(es, "qg_sb", [128, 1], F32)
            kg_sb = sb(es, "kg_sb", [128, 1], F32)
            esink = sb(es, "esink", [64, 8], F32)
            onesb = sb(es, "onesb", [128, 64], BF16)
            epsq = sb(es, "epsq", [128, 1], F32)

            DMA("pool", w_in_sb[:], w_in_l[:], "w", pwrites=["W"])
            DMA("pool", woa_sb[:], woa_l[:], "w", pwrites=["W"])
            DMA("pool", wob_sb[:], wob_l[:], "w", pwrites=["W"])
            DMA("pool", wq_sb[:], wq_l[:], "w", pwrites=["W"])
            DMA("pool", skT_sb[:], skT_l[:], "w", pwrites=["W"])
            DMA("pool", poolw_sb[:], poolw_l[:], "w", pwrites=["W"])
            DMA("sp", bias_sb[:], biasT[:], "w", pwrites=["W"])
            DMA("sp", pinv_sb[:], pool_inv[:], "w", pwrites=["W"])
            DMA("pool", cblk_sb[:], cblk[:], "w", pwrites=["W"])
            DMA("sp", i16_sb[:], ci16[:], "w", pwrites=["W"])
            DMA("sp", pscale_sb[:], pscale_l[:], "w", pwrites=["W"])
            DMA("sp", g1_sb[:], g1_l[:], "w", pwrites=["W"])
            DMA("sp", qg_sb[:], qg_l[:], "w", pwrites=["W"])
            DMA("sp", kg_sb[:], kg_l[:], "w", pwrites=["W"])
            DMA("sp", esink[:], sink_l[:], "w", pwrites=["W"])
            for k_ in ("w_in_sb", "woa_sb", "wob_sb", "wq_sb", "skT_sb", "poolw_sb", "bias_sb", "pinv_sb",
                       "cblk_sb", "i16_sb", "pscale_sb", "g1_sb", "qg_sb"):
                P.alias[k_] = "W"
            ACTF(esink[:], esink[:], AF.Exp, reads=["W"], writes=["esink"])
            TS("dve", kg_sb[:], kg_sb[:], 8.0, None, ALU.mult, None, reads=["W"], writes=["kg_sb"])
            P.c("pool", lambda e: e.memset(onesb[:], 1.0), writes=["onesb"])
            P.c("pool", lambda e: e.memset(epsq[:], 64.0 * EPS), writes=["epsq"])
            epsd = sb(es, "epsd", [128, 1], F32)
            P.c("pool", lambda e: e.memset(epsd[:], EPS), writes=["epsd"])

            xt = [sb(es, f"xt{i}", [128, D], F32) for i in range(2)]
            ssq = [sb(es, f"ssq{i}", [128, 1], F32) for i in range(2)]
            hb = sb(es, "hb", [128, D], BF16)
            hT = sb(es, "hT", [128, 8, 128], BF16)
            sq = sb(es, "sq", [128, 6, 128], BF16)
            rs = sb(es, "rs", [128, 6, 128], F32)
            qnT = sb(es, "qnT", [128, 4, 128], BF16)
            knT = [sb(es, f"knT{i}", [128, 2, 128], BF16) for i in range(2)]
            vtok = [sb(es, f"vtok{i}", [128, 128], BF16) for i in range(2)]
            sfull = sb(es, "sfull", [128, 2, 4, 128], F32)
            PT = sb(es, "PT", [128, 2, 4, 128], BF16)
            rden = sb(es, "rden", [64, 512], F32)
            aT = sb(es, "aT", [64, 8, 128], BF16)
            pbuf = [sb(es, f"pbuf{i}", [128, 4, 144], F32) for i in range(2)]
            sA = sb(es, "sA", [128, 4, 144], F32)
            sB = sb(es, "sB", [128, 4, 144], F32)
            ptmp = sb(es, "ptmp", [128, 4, 16], F32)
            dT = sb(es, "dT", [128, 4, 128], BF16)
            bT = sb(es, "bT", [128, 4, 128], BF16)
            x1 = [sb(es, "x1_0", [128, D], F32)] * 2
            h2b = sb(es, "h2b", [128, D], BF16)
            h2T = [sb(es, f"h2T{i}", [128, 8, 128], BF16) for i in range(2)]
            qpT = sb(es, "qpT", [128, 16, 128], BF16)
            scd = [sb(es, f"sc{i}", [128, 16, 128], F32) for i in range(2)]
            rst = [sb(es, f"rst{i}", [128, 3, 128], BF16) for i in range(2)]
            v16 = sb(es, "v16", [128, 16, 16], F32)
            i16ud = [sb(es, f"i16u{i}", [128, 16, 16], U32) for i in range(2)]
            i16f = sb(es, "i16f", [128, 16, 16], F32)
            cand = sb(es, "cand", [128, 8, 256], F32)
            bestd = [sb(es, f"best{i}", [128, 8, 16], F32) for i in range(2)]
            posud = [sb(es, f"posu{i}", [128, 8, 16], U32) for i in range(2)]
            posf = sb(es, "posf", [128, 8, 16], F32)
            r2f = sb(es, "r2f", [128, 8, 16], F32)
            r1f = sb(es, "r1f", [128, 8, 16], F32)
            eq = sb(es, "eq", [128, 8, 16, 16], BF16)
            ej = sb(es, "ej", [128, 3, 128], F32)
            eb = sb(es, "eb", [128, 8, 16], F32)
            zs = sb(es, "zs", [128, 8], F32)

            tp = ps(es, "tp", [128, 8, 128], BF16)
            B = [None] + [ps(es, f"bk{i}", [128, 512], F32) if i != 6 else None for i in range(1, 8)]
            tpz = ps(es, "tpz", [128, 8, 128], BF16)
            ust = [sb(es, f"ust{i}", [128, D], BF16) for i in range(3)]
            utb = [sb(es, f"utb{i}", [128, 8, 128], BF16) for i in range(2)]

            def bank(i):
                return B[i]

            def front(bi):
                b = bi - 1
                s2 = bi % 2
                o2 = 1 - s2
                X = ("xt", s2)
                DMA("sp", xt[s2][:], xh[bi * 128:(bi + 1) * 128, :], f"xld{s2}", writes=[X])
                ACTF(hb[:], xt[s2][:], AF.Square, reads=[X], writes=["hb", ("ssq", 0)],
                     accum_out=ssq[0][:])
                ACTF(ssq[0][:], ssq[0][:], AF.Ln, reads=[("ssq", 0), "epsd"], writes=[("ssq", 0)],
                     scale=1.0 / D, bias=epsd[:, 0:1])
                ACTF(ssq[0][:], ssq[0][:], AF.Exp, reads=[("ssq", 0)], writes=[("ssq", 0)], scale=-0.5)
                P.c("act", lambda e: e.mul(out=hb[:], in_=xt[s2][:], mul=ssq[0][:, 0:1]),
                    reads=[X, ("ssq", 0)], writes=["hb"])
                for dc in range(8):
                    TR(tp[:, dc, :], hb[:, dc * 128:(dc + 1) * 128], identb[:],
                       reads=["hb", "identb"], pwrites=["tp"])
                TTOP("dve", hT[:], tp[:], g1_sb[:].unsqueeze(2).to_broadcast([128, 8, 128]), ALU.mult,
                     reads=["tp", "g1_sb"], writes=["hT"])
                if a_stage < 1:
                    return
                yq, ykv, yp = bank(1), bank(2), bank(3)
                if b >= 0:
                    for c in range(4):
                        for dc in range(8):
                            MM(yq[:, c * 128:(c + 1) * 128], w_in_sb[:, dc, c * 128:(c + 1) * 128],
                               hT[:, dc, :], dc == 0, dc == 7, reads=["hT", "w_in_sb"], pwrites=[("bk", 1)])
                for c in range(2):
                    for dc in range(8):
                        MM(ykv[:, c * 128:(c + 1) * 128], w_in_sb[:, dc, 512 + c * 128:512 + (c + 1) * 128],
                           hT[:, dc, :], dc == 0, dc == 7, reads=["hT", "w_in_sb"], pwrites=[("bk", 2)])
                for dc in range(8):
                    MM(ykv[:, 256:384], hT[:, dc, :], w_in_sb[:, dc, 768:896],
                       dc == 0, dc == 7, reads=["hT", "w_in_sb"], pwrites=[("bk", 2)])
                for c in range(4):
                    for dc in range(8):
                        MM(yp[:, c * 128:(c + 1) * 128], w_in_sb[:, dc, 896 + c * 128:896 + (c + 1) * 128],
                           hT[:, dc, :], dc == 0, dc == 7, reads=["hT", "w_in_sb"], pwrites=[("bk", 3)])
                if a_stage < 2:
                    return
                sqf = sq[:].rearrange("p a b -> p (a b)")
                rsf = rs[:].rearrange("p a b -> p (a b)")
                ssqb, sskb = bank(4), bank(4)
                if b >= 0:
                    ACTF(sqf[:, 0:512], yq[:], AF.Square, reads=[("bk", 1)], writes=["sq_q"])
                    MM(ssqb[:], cblk_sb[:], sqf[:, 0:512], True, True, reads=["sq_q", "cblk_sb"],
                       writes=[("bk", 4)])
                    ACTF(rsf[:, 0:512], ssqb[:], AF.Ln, reads=[("bk", 4), "epsq"], writes=["rs_q"],
                         bias=epsq[:, 0:1])
                    ACTF(rsf[:, 0:512], rsf[:, 0:512], AF.Exp, reads=["rs_q"], writes=["rs_q"], scale=-0.5)
                    STT("dve", qnT[:].rearrange("p a b -> p (a b)"), yq[:], qg_sb[:, 0:1], rsf[:, 0:512],
                        ALU.mult, ALU.mult, reads=[("bk", 1), "rs_q", "qg_sb"], writes=["qnT"])
                ACTF(sqf[:, 512:768], ykv[:, 0:256], AF.Square, reads=[("bk", 2)], writes=["sq_k"])
                MM(sskb[:, 0:256], cblk_sb[:], sqf[:, 512:768], True, True, reads=["sq_k", "cblk_sb"],
                   writes=[("bk", 4)])
                ACTF(rsf[:, 512:768], sskb[:, 0:256], AF.Ln, reads=[("bk", 4), "epsq"], writes=["rs_k"],
                     bias=epsq[:, 0:1])
                ACTF(rsf[:, 512:768], rsf[:, 512:768], AF.Exp, reads=["rs_k"], writes=["rs_k"], scale=-0.5)
                KN = ("knT", s2)
                STT("dve", knT[s2][:].rearrange("p a b -> p (a b)"), ykv[:, 0:256], kg_sb[:, 0:1],
                    rsf[:, 512:768], ALU.mult, ALU.mult, reads=[("bk", 2), "rs_k", "kg_sb"], writes=[KN])
                VT = ("vtok", s2)
                COPY("act", vtok[s2][:], ykv[:, 256:384], reads=[("bk", 2)], writes=[VT])
                PB = ("pbuf", s2)
                COPY("act", pbuf[s2][:, :, 16:144], yp[:].rearrange("p (a b) -> p a b", a=4),
                     reads=[("bk", 3)], pwrites=[PB])
                if bi == 0:
                    return
                COPY("pool", pbuf[s2][:, :, 0:16], pbuf[o2][:, :, 128:144],
                     reads=[("pbuf", o2)], pwrites=[PB])
                if a_stage < 3:
                    return
                SL = [0, 2, 1, 3]
                for g in range(2):
                    bpar = [bank(3), bank(2)]
                    bpk = [3, 2]
                    for kb in range(2):
                        ks = o2 if kb == 0 else s2
                        for s_ in (0, 2, 1, 3):
                            hh = SL[s_]
                            h = g * 4 + hh
                            r0 = (h % 2) * 64
                            par = s_ // 2
                            col = kb * 256 + (s_ % 2) * 128
                            MM(bpar[par][:, col:col + 128], knT[ks][r0:r0 + 64, g, :],
                               qnT[r0:r0 + 64, h // 2, :], True, True,
                               reads=[("knT", ks), "qnT"], pwrites=[("bk", bpk[par])])
                    for par in range(2):
                        bv = bpar[par][:].rearrange("p (k s q) -> p k s q", k=2, s=2)
                        hs0 = g * 4 + par * 2
                        if bi != 1:
                            TTOP("dve", sfull[:, :, par * 2:par * 2 + 2, :], bv,
                                 bias_sb[:, 0:2, hs0:hs0 + 2, :], ALU.add,
                                 reads=[("bk", bpk[par]), "bias_sb"], pwrites=["sfull"])
                        else:
                            for kb in range(2):
                                TTOP("dve", sfull[:, kb, par * 2:par * 2 + 2, :], bv[:, kb],
                                     bias_sb[:, 2 if kb == 0 else 1, hs0:hs0 + 2, :], ALU.add,
                                     reads=[("bk", bpk[par]), "bias_sb"], pwrites=["sfull"])
                    ACTF(PT[:].rearrange("p k s q -> p (k s q)"), sfull[:].rearrange("p k s q -> p (k s q)"),
                         AF.Exp, reads=["sfull"], writes=["PT"])
                    num, den = bank(4), bank(1)
                    for kb in range(2):
                        ks = o2 if kb == 0 else s2
                        MM(num[0:64, :], vtok[ks][:, g * 64:(g + 1) * 64],
                           PT[:, kb].rearrange("p s q -> p (s q)"), kb == 0, kb == 1,
                           reads=[("vtok", ks), "PT"], pwrites=[("bk", 4)])
                    for kb in range(2):
                        MM(den[0:64, :], onesb[:, 0:64], PT[:, kb].rearrange("p s q -> p (s q)"),
                           kb == 0, kb == 1, reads=["onesb", "PT"], pwrites=[("bk", 1)])
                    for s_ in range(4):
                        h = g * 4 + SL[s_]
                        ACTF(rden[:, s_ * 128:(s_ + 1) * 128], den[0:64, s_ * 128:(s_ + 1) * 128], AF.Ln,
                             reads=[("bk", 1), "esink"], pwrites=["rden"], bias=esink[:, h:h + 1])
                    ACTF(rden[:], rden[:], AF.Exp, reads=["rden"], writes=["rden"], scale=-1.0)
                    TTOP("dve", aT[:, g * 4:(g + 1) * 4, :].rearrange("p a b -> p (a b)"), num[0:64, :],
                         rden[:], ALU.mult, reads=[("bk", 4), "rden"], pwrites=["aT"])
                if a_stage < 4:
                    return
                pb = pbuf[s2]
                TTOP("pool", sA[:, :, 1:144], pb[:, :, 1:144], pb[:, :, 0:143], ALU.add,
                     reads=[PB], writes=["sA"])
                WIN = [2.0, 4.0, 8.0, 16.0]

                def pool_out(g, S, key):
                    if bi == 1:
                        TTOP("pool", ptmp[:, g, :], S[:, g, 16:32], pinv_sb[:, g, :], ALU.mult,
                             reads=[key, "pinv_sb"], pwrites=["ptmp"])
                    TS("pool", S[:, g, 16:144], S[:, g, 16:144], 1.0 / WIN[g], None, ALU.mult, None,
                       reads=[key, "ptmp"], writes=[key])
                    TTOP("pool", dT[:, g, :], S[:, g, 16:144], pb[:, g, 16:144], ALU.subtract,
                         reads=[key, PB], pwrites=["dT"])
                    if bi == 1:
                        TTOP("pool", dT[:, g, 0:16], ptmp[:, g, :], pb[:, g, 16:32], ALU.subtract,
                             reads=["ptmp", PB, "dT"], pwrites=["dT"])
                pool_out(0, sA, "sA")
                TTOP("pool", sB[:, 1:4, 3:144], sA[:, 1:4, 3:144], sA[:, 1:4, 1:142], ALU.add,
                     reads=["sA"], writes=["sB"])
                pool_out(1, sB, "sB")
                TTOP("pool", sA[:, 2:4, 7:144], sB[:, 2:4, 7:144], sB[:, 2:4, 3:140], ALU.add,
                     reads=["sB"], writes=["sA"])
                pool_out(2, sA, "sA")
                TTOP("pool", sB[:, 3:4, 15:144], sA[:, 3:4, 15:144], sA[:, 3:4, 7:136], ALU.add,
                     reads=["sA"], writes=["sB"])
                pool_out(3, sB, "sB")
                po = bank(3)
                for g in range(4):
                    MM(po[:, g * 128:(g + 1) * 128], poolw_sb[:, g, :], dT[:, g, :], True, True,
                       reads=["poolw_sb", "dT"], pwrites=[("bk", 3)])
                TTOP("dve", bT[:], po[:].rearrange("p (a b) -> p a b", a=4),
                     pscale_sb[:].unsqueeze(2).to_broadcast([128, 4, 128]), ALU.mult,
                     reads=[("bk", 3), "pscale_sb"], writes=["bT"])
                if a_stage < 5:
                    return
                X1 = ("x1", 0)
                for half in range(2):
                    wo = bank(1 + half)
                    cs = slice(half * 512, (half + 1) * 512)
                    for h in range(8):
                        MM(wo[:], aT[:, h, :], woa_sb[:, h, cs], h == 0, False,
                           reads=["aT", "woa_sb"], pwrites=[("bk", 1 + half)])
                    for g in range(4):
                        MM(wo[:], bT[:, g, :], wob_sb[:, g, cs], False, g == 3,
                           reads=["bT", "wob_sb"], pwrites=[("bk", 1 + half)])
                    TTOP("dve", x1[s2][:, cs], wo[:], xt[s2][:, cs], ALU.add,
                         reads=[("bk", 1 + half), X], pwrites=[X1])
                DMA("sp", x1s[b * 128:(b + 1) * 128, :], x1[s2][:], "x1st", reads=[X1],
                    pwrites=["x1s"])
                if a_stage < 6:
                    return
                ACTF(h2b[:], x1[s2][:], AF.Square, reads=[X1], writes=["h2b", ("ssq", 1)],
                     accum_out=ssq[1][:])
                ACTF(ssq[1][:], ssq[1][:], AF.Ln, reads=[("ssq", 1), "epsd"], writes=[("ssq", 1)],
                     scale=1.0 / D, bias=epsd[:, 0:1])
                ACTF(ssq[1][:], ssq[1][:], AF.Exp, reads=[("ssq", 1)], writes=[("ssq", 1)], scale=-0.5)
                P.c("act", lambda e: e.mul(out=h2b[:], in_=x1[s2][:], mul=ssq[1][:, 0:1]),
                    reads=[X1, ("ssq", 1)], writes=["h2b"])
                for dc in range(8):
                    TR(tp[:, dc, :], h2b[:, dc * 128:(dc + 1) * 128], identb[:],
                       reads=["h2b", "identb"], pwrites=["tp"])
                H2 = ("h2T", s2)
                TTOP("dve", h2T[s2][:], tp[:], g2_sb[:].unsqueeze(2).to_broadcast([128, 8, 128]), ALU.mult,
                     reads=["tp", "g2_sb"], writes=[H2])
                DMA("sp", h2s[b // 2][:, :, (b % 2) * 128:(b % 2 + 1) * 128], h2T[s2][:], "h2st",
                    reads=[H2], pwrites=["h2s"])

            def front3(bi):
                b = bi - 1
                s2 = bi % 2
                H2 = ("h2T", s2)
                if a_stage < 7:
                    return
                scs = bi % 2
                sc = scd[scs]
                qbanks = [7, 7, 7, 7]
                for grp in range(4):
                    qb = bank(qbanks[grp])
                    for cc in range(4):
                        c = grp * 4 + cc
                        for dc in range(8):
                            MM(qb[:, cc * 128:(cc + 1) * 128], wq_sb[:, dc, c * 128:(c + 1) * 128],
                               h2T[s2][:, dc, :], dc == 0, dc == 7, reads=[H2, "wq_sb"],
                               pwrites=[("bk", qbanks[grp])])
                    COPY("act", qpT[:, grp * 4:(grp + 1) * 4, :].rearrange("p a b -> p (a b)"), qb[:],
                         reads=[("bk", qbanks[grp])], pwrites=["qpT"])
                sbanks = [7, 7, 7, 7]
                for grp in range(4):
                    sbk = bank(sbanks[grp])
                    for cc in range(4):
                        c = grp * 4 + cc
                        MM(sbk[:, cc * 128:(cc + 1) * 128], qpT[:, c, :], skT_sb[:, c, :], True, True,
                           reads=["qpT", "skT_sb"], pwrites=[("bk", sbanks[grp])])
                    COPY("act", sc[:, grp * 4:(grp + 1) * 4, :].rearrange("p a b -> p (a b)"), sbk[:],
                         reads=[("bk", sbanks[grp])], writes=[("sc", scs, grp * 4 + cc) for cc in range(4)])

            def route1(bi):
                b = bi - 1
                scs = bi % 2
                sc = scd[scs]
                best, posu, i16u = bestd[scs], posud[scs], i16ud[scs]
                if a_stage < 8:
                    return
                for c in range(16):
                    P.c("dve", (lambda c=c: lambda e: e.max(out=v16[:, c, 0:8], in_=sc[:, c, :]))(),
                        reads=[("sc", scs, c)], writes=[("v16a", c)], cost=200)
                for c in range(16):
                    P.c("dve", (lambda c=c: lambda e: e.max_index(
                        out=i16u[:, c, 0:8], in_max=v16[:, c, 0:8], in_values=sc[:, c, :]))(),
                        reads=[("sc", scs, c), ("v16a", c)], pwrites=[("i16u", scs)], cost=270)
                for c in range(16):
                    P.c("dve", (lambda c=c: lambda e: e.match_replace(
                        out=sc[:, c, :], in_to_replace=v16[:, c, 0:8], in_values=sc[:, c, :], imm_value=-1e30))(),
                        reads=[("sc", scs, c), ("v16a", c)], writes=[("sc", scs, c)], cost=270)
                for c in range(16):
                    P.c("dve", (lambda c=c: lambda e: e.max(out=v16[:, c, 8:16], in_=sc[:, c, :]))(),
                        reads=[("sc", scs, c)], writes=[("v16b", c)], cost=200)
                for c in range(16):
                    P.c("dve", (lambda c=c: lambda e: e.max_index(
                        out=i16u[:, c, 8:16], in_max=v16[:, c, 8:16], in_values=sc[:, c, :]))(),
                        reads=[("sc", scs, c), ("v16b", c)], pwrites=[("i16u", scs)], cost=270)
                v4 = v16[:].rearrange("p (h q) r -> p h q r", q=2)
                TTOP("dve", cand[:].rearrange("p h (a b) -> p h a b", a=16),
                     v4[:, :, 0, :].unsqueeze(3).to_broadcast([128, 8, 16, 16]),
                     v4[:, :, 1, :].unsqueeze(2).to_broadcast([128, 8, 16, 16]), ALU.add,
                     reads=[("v16a", c) for c in range(16)] + [("v16b", c) for c in range(16)],
                     writes=[("cand", h) for h in range(8)])
                for h in range(8):
                    P.c("dve", (lambda h=h: lambda e: e.max(out=best[:, h, 0:8], in_=cand[:, h, :]))(),
                        reads=[("cand", h)], writes=[("besta", scs, h)], cost=300)
                for h in range(8):
                    P.c("dve", (lambda h=h: lambda e: e.max_index(
                        out=posu[:, h, 0:8], in_max=best[:, h, 0:8], in_values=cand[:, h, :]))(),
                        reads=[("cand", h), ("besta", scs, h)], pwrites=[("posu", scs)], cost=330)
                for h in range(8):
                    P.c("dve", (lambda h=h: lambda e: e.match_replace(
                        out=cand[:, h, :], in_to_replace=best[:, h, 0:8], in_values=cand[:, h, :],
                        imm_value=-1e30))(), reads=[("cand", h), ("besta", scs, h)], writes=[("cand", h)], cost=330)
                for h in range(8):
                    P.c("dve", (lambda h=h: lambda e: e.max(out=best[:, h, 8:16], in_=cand[:, h, :]))(),
                        reads=[("cand", h)], writes=[("bestb", scs, h)], cost=300)
                for h in range(8):
                    P.c("dve", (lambda h=h: lambda e: e.max_index(
                        out=posu[:, h, 8:16], in_max=best[:, h, 8:16], in_values=cand[:, h, :]))(),
                        reads=[("cand", h), ("bestb", scs, h)], pwrites=[("posu", scs)], cost=330)

            def route2(bi):
                b = bi - 1
                scs = bi % 2
                best, posu, i16u = bestd[scs], posud[scs], i16ud[scs]
                if a_stage < 9:
                    return
                TTOP("dve", eb[:], best[:], best[:, :, 0:1].to_broadcast([128, 8, 16]), ALU.subtract,
                     reads=[("besta", scs, h) for h in range(8)] + [("bestb", scs, h) for h in range(8)], writes=["eb"])
                ACTF(eb[:], eb[:], AF.Exp, reads=["eb"], writes=["eb"])
                P.c("dve", lambda e: e.tensor_reduce(out=zs[:], in_=eb[:], axis=AX.X, op=ALU.add),
                    reads=["eb"], writes=["zs"])
                RECIP(zs[:], zs[:], reads=["zs"], writes=["zs"])
                TTOP("dve", ej[:, 2, :].rearrange("p (h k) -> p h k", h=8), eb[:],
                     zs[:].unsqueeze(2).to_broadcast([128, 8, 16]), ALU.mult,
                     reads=["eb", "zs"], pwrites=["ej"])
                if a_stage < 10:
                    return
                COPY("pool", i16f[:], i16u[:], reads=[("i16u", scs)], writes=["i16f"])
                P.c("dve", lambda e: e.tensor_single_scalar(out=posf[:].bitcast(U32), in_=posu[:], scalar=4,
                                                            op=ALU.logical_shift_right),
                    reads=[("posu", scs)], writes=["posf"])
                COPY("pool", r1f[:], posf[:].bitcast(U32), reads=["posf"], writes=["r1f"])
                P.c("dve", lambda e: e.tensor_single_scalar(out=posf[:].bitcast(U32), in_=posu[:], scalar=15,
                                                            op=ALU.bitwise_and),
                    reads=[("posu", scs), "r1f"], writes=["posf"])
                COPY("pool", r2f[:], posf[:].bitcast(U32), reads=["posf"], writes=["r2f"])
                i4 = i16f[:].rearrange("p (h q) r -> p h q r", q=2)
                for which, rf, key in ((0, r1f, "r1f"), (1, r2f, "r2f")):
                    TTOP("dve", eq[:], rf[:].unsqueeze(3).to_broadcast([128, 8, 16, 16]),
                         i16_sb[:, 1, :].unsqueeze(1).unsqueeze(1).to_broadcast([128, 8, 16, 16]),
                         ALU.is_equal, reads=[key, "i16_sb"], writes=["eq"])
                    TTOP("dve", eq[:], eq[:], i4[:, :, which, :].unsqueeze(2).to_broadcast([128, 8, 16, 16]),
                         ALU.mult, reads=["eq", "i16f"], writes=["eq"])
                    P.c("dve", (lambda which=which: lambda e: e.tensor_reduce(
                        out=ej[:, which, :].rearrange("p (h k) -> p h k", h=8), in_=eq[:], axis=AX.X,
                        op=ALU.add))(), reads=["eq"], pwrites=["ej"], cost=2290)
                if a_stage < 11:
                    return
                rb = bank(5)
                for w3 in range(3):
                    TR(rb[:, w3 * 128:(w3 + 1) * 128], ej[:, w3, :], identf[:], reads=["ej", "identf"],
                       pwrites=[("bk", 5)])
                COPY("act", rst[scs][:], rb[:, 0:384].rearrange("p (a b) -> p a b", a=3),
                     reads=[("bk", 5)], writes=[("rst", scs)])
                DMA("sp", Rs[b // 2][:, :, (b % 2) * 128:(b % 2 + 1) * 128], rst[scs][:], f"rst{scs}",
                    reads=[("rst", scs)], pwrites=["Rs"])

            def captured(fn, *a_):
                P.capture = []
                fn(*a_)
                ops_ = P.capture
                P.capture = None
                return ops_

            def zconv(bi):
                if only_a or bi < 1:
                    return
                for e1 in range((bi - 1) * 4, bi * 4):
                    s_ = e1 % 3
                    t_ = e1 % 2
                    DMA("pool", ust[s_][:], peer_u[e1 * 128:(e1 + 1) * 128, :], f"uld{s_}",
                        writes=[("ust", s_)])
                    for dc in range(8):
                        TR(tpz[:, dc, :], ust[s_][:, dc * 128:(dc + 1) * 128], identb[:],
                           reads=[("ust", s_), "identb"], pwrites=["tpz"])
                    COPY("act", utb[t_][:], tpz[:], reads=["tpz"], writes=[("utb", t_)])
                    DMA("sp", UTs[e1], utb[t_][:].rearrange("p a b -> p (a b)"), f"ust{t_}",
                        reads=[("utb", t_)], pwrites=["UTs"])

            if not only_a:
                for i in range(64):
                    DMA("pool", Vs[i * 256:(i + 1) * 256, :], peer_v[i * 256:(i + 1) * 256, :],
                        "vc", pwrites=["Vs"])
            F3p, T1p, T2p = [], [], []
            for bi in range(nblk_run + 1):
                Fops = captured(front, bi)
                Zops = captured(zconv, bi)
                P.schedule(Fops, F3p, T1p, T2p, Zops)
                T2p = captured(route2, bi - 2) if bi >= 3 else []
                T1p = captured(route1, bi - 1) if bi >= 2 else []
                F3p = captured(front3, bi) if bi >= 1 else []
            nb_ = nblk_run
            P.schedule(F3p, T1p, T2p)
            if nb_ >= 1:
                P.schedule(captured(route1, nb_), captured(route2, nb_ - 1) if nb_ >= 2 else [])
                P.schedule(captured(route2, nb_))
            P.flush()
            if stop_after == "A":
                return nc

        with ExitStack() as es:
            Gt = [sb(es, f"Gt{i}", [128, 128, TT], BF16) for i in range(2)]
            A_oh = [sb(es, f"A_oh{i}", [128, 4, 128], BF16) for i in range(2)]
            B_oh = [sb(es, f"B_oh{i}", [128, 4, 128], BF16) for i in range(2)]
            NS = 5
            ubuf = [sb(es, f"ubuf{i}", [128, 8, 128], BF16) for i in range(NS)]
            vbuf = [sb(es, f"vbuf{i}", [128, D], BF16) for i in range(NS)]
            h2t = [sb(es, f"h2t{i}", [128, 8, TT], BF16) for i in range(2)]
            gl = [sb(es, f"gl{i}", [128, TT], BF16) for i in range(3)]
            GA = [sb(es, f"GA{i}", [128, TT], BF16) for i in range(3)]
            x1t = sb(es, "x1t", [128, 2, D], F32)
            ot = sb(es, "ot", [128, 2, D], F32)
            acc = [[ps(es, f"acc{a}{d}", [128, 512], F32) for d in range(2)] for a in range(2)]
            actp = [ps(es, f"actp{i}", [128, 512], F32) for i in range(3)]
            gps = [ps(es, "gps0", [128, 4, 128], F32)] * 2

            nsteps = NTILE * 128

            def load_step(sidx):
                if sidx >= nsteps:
                    return
                e1 = sidx % 128
                sl = sidx % NS
                DMA("sp", ubuf[sl][:].rearrange("p a b -> p (a b)"), UTs[e1], f"ul{sl}",
                    reads=["UTs"], writes=[("ubuf", sl)])
                DMA("sp", vbuf[sl][:], Vs[e1 * 128:(e1 + 1) * 128, :], f"vl{sl}",
                    reads=["Vs"], writes=[("vbuf", sl)])

            Rt = [sb(es, f"Rt{i}", [128, 3, TT], BF16) for i in range(2)]

            def load_rt(tile):
                if tile >= NTILE:
                    return
                DMA("sp", Rt[tile % 2][:], Rs[tile], f"rtl{tile % 2}", reads=["Rs"], writes=[("Rt", tile % 2)])

            def g_oh(tile, tl):
                if tile >= NTILE:
                    return
                rt = Rt[tile % 2]
                q4, k = tl // 4, tl % 4
                sl = q4 % 2
                TS("dve", A_oh[sl][:, k, :], iotab[:], rt[:, 0, tl:tl + 1], rt[:, 2, tl:tl + 1],
                   ALU.is_equal, ALU.mult, reads=["iotab", ("Rt", tile % 2)], writes=[("A_oh", sl, k)])
                TS("dve", B_oh[sl][:, k, :], iotab[:], rt[:, 1, tl:tl + 1], None,
                   ALU.is_equal, None, reads=["iotab", ("Rt", tile % 2)], writes=[("B_oh", sl, k)])

            def g_mm(tile, tl):
                if tile >= NTILE:
                    return
                q4, k = tl // 4, tl % 4
                sl = q4 % 2
                MM(gps[sl][:, k, :], B_oh[sl][:, k, :], A_oh[sl][:, k, :], True, True,
                   reads=[("A_oh", sl, k), ("B_oh", sl, k)], pwrites=[("gps", 0)])

            def g_ev(tile, q4):
                if tile >= NTILE:
                    return
                sl = q4 % 2
                t4 = q4 * 4
                COPY("act", Gt[tile % 2][:, :, t4:t4 + 4], gps[sl][:].rearrange("p t e -> p e t"),
                     reads=[("gps", 0)], pwrites=[("Gt", tile % 2)])

            for pre in range(NS):
                load_step(pre)
            load_rt(0)
            for tl in range(TT):
                g_oh(0, tl)
                g_mm(0, tl)
                if tl % 4 == 3:
                    g_ev(0, tl // 4)

            for tau in range(NTILE):
                tok0 = tau * TT
                hs = tau % 2
                gb = tau % 2
                DMA("sp", h2t[hs][:], h2s[tau], f"h2l{hs}", reads=["h2s"], writes=[("h2t", hs)])
                load_rt(tau + 1)

                def u_stage(e1):
                    sidx = tau * 128 + e1
                    sl = sidx % NS
                    a = e1 % 3
                    for dc in range(8):
                        MM(actp[a][:, 0:TT], ubuf[sl][:, dc, :], h2t[hs][:, dc, :], dc == 0, dc == 7,
                           reads=[("ubuf", sl), ("h2t", hs)], pwrites=[("actp", a)])
                    ACTF(gl[a][:], actp[a][:, 0:TT], AF.Gelu, reads=[("actp", a)], writes=[("gl", a)])
                    TTOP("dve", GA[a][:], gl[a][:], Gt[gb][:, e1, :], ALU.mult,
                         reads=[("gl", a), ("Gt", gb)], writes=[("GA", a)])

                def v_stage(e1):
                    sidx = tau * 128 + e1
                    sl = sidx % NS
                    a = e1 % 3
                    for ts_ in range(2):
                        for dh in range(2):
                            MM(acc[ts_][dh][:], GA[a][:, ts_ * 128:(ts_ + 1) * 128],
                               vbuf[sl][:, dh * 512:(dh + 1) * 512], e1 == 0, e1 == 127,
                               reads=[("GA", a), ("vbuf", sl)], pwrites=[("acc", ts_, dh)])

                for e1 in range(128):
                    u_stage(e1)
                    if e1 >= 2:
                        v_stage(e1 - 2)
                        load_step(tau * 128 + e1 - 2 + NS)
                    g_oh(tau + 1, 2 * e1)
                    g_oh(tau + 1, 2 * e1 + 1)
                    if e1 >= 3 and e1 % 2 == 1:
                        g_ev(tau + 1, (e1 - 3) // 2)
                    if e1 >= 1:
                        g_mm(tau + 1, 2 * e1 - 2)
                        g_mm(tau + 1, 2 * e1 - 1)
                v_stage(126)
                load_step(tau * 128 + 126 + NS)
                v_stage(127)
                load_step(tau * 128 + 127 + NS)
                g_mm(tau + 1, 254)
                g_mm(tau + 1, 255)
                g_ev(tau + 1, 63)
                DMA("sp", x1t[:], x1s[tok0:tok0 + TT, :].rearrange("(a p) d -> p a d", p=128),
                    "x1l", reads=["x1s"], writes=["x1t"])
                for ts_ in range(2):
                    for dh in range(2):
                        TTOP("dve", ot[:, ts_, dh * 512:(dh + 1) * 512], acc[ts_][dh][:],
                             x1t[:, ts_, dh * 512:(dh + 1) * 512], ALU.add,
                             reads=[("acc", ts_, dh), "x1t"], pwrites=["ot"])
                DMA("sp", out[tok0:tok0 + TT, :].rearrange("(a p) d -> p a d", p=128), ot[:], "ost",
                    reads=["ot"], pwrites=["out"])
            P.flush()
    return nc


def _t5_bucket(rel):
    nb = 16
    max_exact = 8
    base = np.where(rel > 0, nb, 0)
    n = np.abs(rel)
    nf = np.maximum(n, 1).astype(np.float32)
    large = max_exact + (np.log(nf / np.float32(max_exact)) / np.float32(np.log(128 / max_exact))
                         * np.float32(nb - max_exact)).astype(np.int32)
    large = np.minimum(large, nb - 1)
    return base + np.where(n < max_exact, n, large)


HORD = [0, 2, 1, 3, 4, 6, 5, 7]


def _bias_tables(rel_bias, seq_start):
    ki = np.arange(128)[:, None]
    qi = np.arange(128)[None, :]
    cq = qi // 64
    out = np.empty((128, 3, 8, 128), np.float32)
    for var, off in ((0, -128), (1, 0)):
        rel = (off + ki) - qi
        bk = _t5_bucket(rel)
        ckr = ki // 64 + (off // 64)
        valid = (ckr >= cq - 2) & (ckr <= cq)
        vals = rel_bias[bk]
        vals = np.where(valid[:, :, None], vals, np.float32(NEG))
        out[:, var] = np.transpose(vals, (0, 2, 1))[:, HORD, :]
    out[:, 2] = np.float32(NEG) if seq_start else out[:, 0]
    return out


_NC_CACHE = {}


def kernel(x, norm1_g, w_in, q_norm_g, k_norm_g, attn_sinks, rel_bias, pool_w, pool_scale,
           w_out, norm2_g, peer_wq, peer_subkeys, peer_u, peer_v, _debug=False, _stop=None, _dbgA=None):
    f = np.float32
    x = np.asarray(x, f)
    w_in0 = np.asarray(w_in, f)[0]
    wext = np.concatenate([w_in0[:, 0:512], w_in0[:, 512:576], w_in0[:, 512:576],
                           w_in0[:, 576:640], w_in0[:, 576:640], w_in0[:, 640:768],
                           w_in0[:, 768:1280]], axis=1)

    def pmaj(w):
        return np.ascontiguousarray(w.reshape(8, 128, -1).transpose(1, 0, 2))

    w_out0 = np.asarray(w_out, f)[0]
    shared = {
        "cident": np.eye(128, dtype=f),
        "ciota": np.tile(np.arange(128, dtype=f), (128, 1)),
        "cblk": np.kron(np.eye(2, dtype=f), np.ones((64, 64), f)),
        "ci16": np.tile(np.stack([16.0 * np.arange(16), np.arange(16)]).astype(f)[None], (128, 1, 1)),
        "w_in_l": pmaj(wext),
        "woa_l": np.ascontiguousarray(w_out0[0:512].reshape(8, 64, 1024)[HORD].transpose(1, 0, 2)),
        "wob_l": np.ascontiguousarray(w_out0[512:].reshape(4, 128, 1024).transpose(1, 0, 2)),
        "wq_l": pmaj(np.asarray(peer_wq, f)[0]),
        "skT_l": np.ascontiguousarray(np.asarray(peer_subkeys, f)[0].reshape(16, 128, 128).transpose(2, 0, 1)),
        "poolw_l": np.ascontiguousarray(np.asarray(pool_w, f)[0].transpose(1, 0, 2)),
        "pscale_l": np.ascontiguousarray(np.asarray(pool_scale, f)[0].reshape(4, 128).T),
        "g1_l": np.ascontiguousarray(np.asarray(norm1_g, f)[0].reshape(8, 128).T),
        "g2_l": np.ascontiguousarray(np.asarray(norm2_g, f)[0].reshape(8, 128).T),
        "qg_l": np.tile(np.asarray(q_norm_g, f)[0], 2).reshape(128, 1),
        "kg_l": np.tile(np.asarray(k_norm_g, f)[0], 2).reshape(128, 1),
        "sink_l": np.tile(np.asarray(attn_sinks, f)[0][None, :], (64, 1)),
        "peer_u": np.asarray(peer_u, f)[0],
        "peer_v": np.asarray(peer_v, f)[0],
    }
    rb = np.asarray(rel_bias, f)
    bias_tabs = {True: _bias_tables(rb, True), False: _bias_tables(rb, False)}
    in_maps = []
    for c in range(NCORES):
        bidx, half = c // 2, c % 2
        start = half * TOK
        xh = np.zeros((TOK + 128, D), f)
        xh[128:] = x[bidx, start:start + TOK]
        if half == 1:
            xh[:128] = x[bidx, start - 128:start]
        pinv = np.empty((128, 4, 16), f)
        for g, w in enumerate((2, 4, 8, 16)):
            t = np.arange(16)
            pinv[:, g, :] = (1.0 / np.minimum(t + 1, w) if half == 0 else np.full(16, 1.0 / w))
        m = dict(shared)
        m["xh"] = xh
        m["biasT"] = bias_tabs[half == 0]
        m["pool_inv"] = pinv
        in_maps.append(m)
    if _dbgA is not None:
        nc = build(debug=True, stop_after="A", only_a=True, nblk_run=_dbgA[0], a_stage=_dbgA[1])
        for m in in_maps:
            del m["peer_u"], m["peer_v"]
        res = run_bass_kernel_spmd(nc, in_maps, core_ids=list(range(NCORES)))
        return None, res
    key = (bool(_debug), _stop)
    if key not in _NC_CACHE:
        _NC_CACHE[key] = build(debug=bool(_debug), stop_after=_stop)
    nc = _NC_CACHE[key]
    res = run_bass_kernel_spmd(nc, in_maps, core_ids=list(range(NCORES)))
    outs = [np.asarray(r["out"]) for r in res.results]
    full = np.stack(outs, 0).reshape(4, 2 * TOK, D).astype(np.float32)
    if _debug:
        return full, res
    return full
```

```python
import numpy as np
import ml_dtypes
from contextlib import ExitStack
import concourse.bass as bass
import concourse.mybir as mybir
from concourse.bass_utils import run_bass_kernel_spmd

F32 = mybir.dt.float32
BF16 = mybir.dt.bfloat16
U32 = mybir.dt.uint32
AF = mybir.ActivationFunctionType
ALU = mybir.AluOpType
AX = mybir.AxisListType

NCORES = 8
TOK = 4096
NBLK = 32
D = 1024
NEG = -30000.0
EPS = 1e-6
TT = 256
NTILE = TOK // TT
CH = 16
COMPUTE = ("pe", "act", "dve", "pool")


class Prog:
    def __init__(self, nc):
        self.nc = nc
        self.eng = {"pe": nc.tensor, "act": nc.scalar, "dve": nc.vector,
                    "pool": nc.gpsimd, "sp": nc.sync}
        self.ops = []
        self.res = {}
        self.sems = {}
        self.sig_total = {e: 0 for e in COMPUTE}
        self.dma_total = {}
        self.dma_waited = {e: {} for e in self.eng}
        self.nops_total = 0
        self.alias = {}
        self.capture = None
        self.fin = []
        self.efree = {}

    def _sem(self, key):
        if key not in self.sems:
            nm = "s_" + "_".join(str(k) for k in key)
            self.sems[key] = self.nc.alloc_semaphore(name=nm)
        return self.sems[key]

    def _resolve(self, reads, writes, pwrites):
        al = self.alias
        return (tuple(al.get(k, k) for k in reads), tuple(al.get(k, k) for k in writes),
                tuple(al.get(k, k) for k in pwrites))

    def _deps(self, reads, writes, pwrites):
        deps = set()
        for k in reads:
            st = self.res.get(k)
            if st is not None:
                deps.update(st["w"])
        for k, partial in [(k, False) for k in writes] + [(k, True) for k in pwrites]:
            st = self.res.get(k)
            if st is None:
                continue
            if st["r"] or k in reads:
                deps.update(st["r"])
                deps.update(st["w"])
            elif partial:
                deps.update(st["g"])
            else:
                deps.update(st["w"])
                deps.update(st["g"])
        return deps

    def _add(self, kind, eng, emit, reads, writes, pwrites, semkey=None, ndma=1, cost=0):
        idx = len(self.ops)
        reads, writes, pwrites = self._resolve(reads, writes, pwrites)
        deps = set()
        for k in reads:
            st = self.res.get(k)
            if st is None:
                st = self.res[k] = {"w": [], "r": [], "g": []}
            deps.update(st["w"])
            st["r"].append(idx)
        for k, partial in [(k, False) for k in writes] + [(k, True) for k in pwrites]:
            st = self.res.get(k)
            if st is None:
                st = self.res[k] = {"w": [], "r": [], "g": []}
            if st["r"]:
                g = list(st["r"]) + list(st["w"])
                deps.update(g)
                st["g"] = g
                st["w"] = [idx]
                st["r"] = []
            elif partial:
                deps.update(st["g"])
                st["w"].append(idx)
            else:
                deps.update(st["w"])
                deps.update(st["g"])
                st["w"] = [idx]
        deps.discard(idx)
        ready = 0.0
        for d in deps:
            f = self.fin[d]
            if self.ops[d]["eng"] != eng or kind == "d" or self.ops[d]["kind"] == "d":
                f += 300.0
            elif eng != "pe":
                f += 170.0
            ready = max(ready, f)
        start = max(ready, self.efree.get(eng, 0.0))
        if kind == "c":
            self.efree[eng] = start + cost
            self.fin.append(start + cost)
        else:
            self.efree[eng] = start + (1500.0 if eng == "pool" else 120.0)
            self.fin.append(start + 2200.0 + cost)
        self.ops.append(dict(kind=kind, eng=eng, emit=emit, deps=deps, semkey=semkey,
                             ndma=ndma, signal=False, waits=[]))
        return idx

    def c(self, eng, emit, reads=(), writes=(), pwrites=(), cost=None):
        if cost is None:
            cost = {"pe": 120.0, "act": 500.0, "dve": 350.0, "pool": 1200.0}.get(eng, 100.0)
        a = ("c", eng, emit, tuple(reads), tuple(writes), tuple(pwrites), None, 1, float(cost))
        if self.capture is not None:
            self.capture.append(a)
            return None
        return self._add(*a)

    def dma(self, eng, emit, semkey, reads=(), writes=(), pwrites=(), ndma=1, cost=1000.0):
        if eng == "pool":
            semkey = semkey + "_sw"
        a = ("d", eng, emit, tuple(reads), tuple(writes), tuple(pwrites), semkey, ndma, float(cost))
        if self.capture is not None:
            self.capture.append(a)
            return None
        return self._add(*a)

    def schedule(self, *lists):
        lists = [l for l in lists if l]
        idx = [0] * len(lists)
        while True:
            best, bt = None, None
            for i, l in enumerate(lists):
                if idx[i] >= len(l):
                    continue
                kind, eng, _, reads, writes, pwrites, _, _, cost = l[idx[i]]
                r, w, pw = self._resolve(reads, writes, pwrites)
                ready = 0.0
                for d in self._deps(r, w, pw):
                    f = self.fin[d] + (300.0 if (self.ops[d]["eng"] != eng or kind == "d"
                                                 or self.ops[d]["kind"] == "d") else 170.0)
                    ready = max(ready, f)
                t = max(ready, self.efree.get(eng, 0.0))
                if bt is None or t < bt - 1e-9:
                    best, bt = i, t
            if best is None:
                break
            self._add(*lists[best][idx[best]])
            idx[best] += 1

    def merge(self, *lists):
        lists = [l for l in lists if l]
        idx = [0] * len(lists)
        while True:
            best, bf = None, None
            for i, l in enumerate(lists):
                if idx[i] < len(l):
                    f = idx[i] / len(l)
                    if bf is None or f < bf:
                        best, bf = i, f
            if best is None:
                break
            self._add(*lists[best][idx[best]])
            idx[best] += 1

    def flush(self):
        ops = self.ops
        pos = {e: 0 for e in COMPUTE}
        by_pos = {}
        last = {}
        for i, op in enumerate(ops):
            if op["kind"] == "c":
                pos[op["eng"]] += 1
                op["pos"] = pos[op["eng"]]
                by_pos[(op["eng"], op["pos"])] = i
                last[op["eng"]] = i
        wpos = {e: {E: 0 for E in COMPUTE} for e in self.eng}
        dma_val = {}
        for i, op in enumerate(ops):
            e = op["eng"]
            need_c = {}
            need_d = {}
            for d in op["deps"]:
                p = ops[d]
                if p["kind"] == "c":
                    if p["eng"] == "pe" and e == "pe" and op["kind"] == "c":
                        continue
                    need_c[p["eng"]] = max(need_c.get(p["eng"], 0), p["pos"])
                else:
                    need_d[p["semkey"]] = max(need_d.get(p["semkey"], 0), dma_val[d])
            for E, v in need_c.items():
                if wpos[e][E] >= v:
                    continue
                wpos[e][E] = v
                op["waits"].append(("c", E, v))
                ops[by_pos[(E, v)]]["signal"] = True
            for k, v in need_d.items():
                if self.dma_waited[e].get(k, 0) >= v:
                    continue
                self.dma_waited[e][k] = v
                op["waits"].append(("d", k, v))
            if op["kind"] == "d":
                k = op["semkey"]
                self.dma_total[k] = self.dma_total.get(k, 0) + 16 * op["ndma"]
                dma_val[i] = self.dma_total[k]
        for E, i in last.items():
            ops[i]["signal"] = True
        cnt = dict(self.sig_total)
        sigcount = {}
        for i, op in enumerate(ops):
            if op["kind"] == "c":
                if op["signal"]:
                    cnt[op["eng"]] += 1
                sigcount[(op["eng"], op["pos"])] = cnt[op["eng"]]
        for op in ops:
            E = self.eng[op["eng"]]
            for kind, k, v in op["waits"]:
                if kind == "c":
                    E.wait_ge(self._sem(("c", k)), sigcount[(k, v)])
                else:
                    E.wait_ge(self._sem(("d", k)), v)
            r = op["emit"](E)
            if op["kind"] == "c":
                if op["signal"]:
                    r.then_inc(self._sem(("c", op["eng"])), 1)
            else:
                rs = r if isinstance(r, (list, tuple)) else [r]
                assert len(rs) == op["ndma"]
                for x in rs:
                    x.then_inc(self._sem(("d", op["semkey"])), 16)
        self.sig_total = cnt
        for e, E in self.eng.items():
            for Ec in COMPUTE:
                if cnt[Ec] > 0 and Ec != e:
                    E.wait_ge(self._sem(("c", Ec)), cnt[Ec])
            for k, v in self.dma_total.items():
                if self.dma_waited[e].get(k, 0) < v:
                    E.wait_ge(self._sem(("d", k)), v)
                    self.dma_waited[e][k] = v
        self.nops_total += len(ops)
        tb = max(list(self.efree.values()) + self.fin + [0.0])
        self.efree = {e: tb for e in self.eng}
        self.fin = []
        self.ops = []
        self.res = {}


def build(debug=False, stop_after=None, only_a=False, nblk_run=NBLK, a_stage=99):
    nc = bass.Bass("TRN2", target_bir_lowering=False)

    def din(name, shape, dt=F32):
        return nc.dram_tensor(name, list(shape), dt, kind="ExternalInput")

    xh = din("xh", [TOK + 128, D])
    biasT = din("biasT", [128, 3, 8, 128])
    pool_inv = din("pool_inv", [128, 4, 16])
    cident = din("cident", [128, 128])
    ciota = din("ciota", [128, 128])
    cblk = din("cblk", [128, 128])
    ci16 = din("ci16", [128, 2, 16])
    w_in_l = din("w_in_l", [128, 8, 1408])
    woa_l = din("woa_l", [64, 8, 1024])
    wob_l = din("wob_l", [128, 4, 1024])
    wq_l = din("wq_l", [128, 8, 2048])
    skT_l = din("skT_l", [128, 16, 128])
    poolw_l = din("poolw_l", [128, 4, 128])
    pscale_l = din("pscale_l", [128, 4])
    g1_l = din("g1_l", [128, 8])
    g2_l = din("g2_l", [128, 8])
    qg_l = din("qg_l", [128, 1])
    kg_l = din("kg_l", [128, 1])
    sink_l = din("sink_l", [64, 8])
    if not only_a:
        peer_u = din("peer_u", [16384, D])
        peer_v = din("peer_v", [16384, D])
    out = nc.dram_tensor("out", [TOK, D], F32, kind="ExternalOutput")

    UTs = nc.dram_tensor("UTs", [128, 128, 1024], BF16, kind="Internal")
    Vs = nc.dram_tensor("Vs", [16384, D], BF16, kind="Internal")
    skind = "ExternalOutput" if debug else "Internal"
    x1s = nc.dram_tensor("x1s", [TOK, D], F32, kind=skind)
    h2s = nc.dram_tensor("h2s", [NTILE, 128, 8, TT], BF16, kind="Internal")
    Rs = nc.dram_tensor("Rs", [NTILE, 128, 3, TT], BF16, kind=skind)

    P = Prog(nc)

    def fs(ap):
        n = 1
        for d_ in ap.shape[1:]:
            n *= int(d_)
        return n

    def ecost(eng, n):
        if eng == "dve":
            return (150.0 + n) / 0.96
        if eng == "act":
            return (224.0 + n) / 1.4
        if eng == "pool":
            return 450.0 + 1.5 * n
        return 120.0

    def DMA(q, out_, in_, semkey, reads=(), writes=(), pwrites=()):
        P.dma(q, lambda e: e.dma_start(out=out_, in_=in_), semkey, reads, writes, pwrites,
              cost=fs(out_) * 0.5)

    def ACTF(out_, in_, func, reads=(), writes=(), pwrites=(), **kw):
        P.c("act", lambda e: e.activation(out=out_, in_=in_, func=func, **kw), reads, writes, pwrites,
            cost=ecost("act", fs(out_)))

    def COPY(eng, out_, in_, reads=(), writes=(), pwrites=()):
        if eng == "act":
            P.c("act", lambda e: e.copy(out=out_, in_=in_), reads, writes, pwrites, cost=ecost("act", fs(out_)))
        else:
            P.c(eng, lambda e: e.tensor_copy(out=out_, in_=in_), reads, writes, pwrites,
                cost=ecost(eng, fs(out_)))

    def TTOP(eng, out_, in0, in1, op, reads=(), writes=(), pwrites=()):
        P.c(eng, lambda e: e.tensor_tensor(out=out_, in0=in0, in1=in1, op=op), reads, writes, pwrites,
            cost=ecost(eng, fs(out_)))

    def TS(eng, out_, in0, s1, s2, op0, op1, reads=(), writes=(), pwrites=()):
        if op1 is None:
            P.c(eng, lambda e: e.tensor_scalar(out=out_, in0=in0, scalar1=s1, scalar2=None, op0=op0),
                reads, writes, pwrites, cost=ecost(eng, fs(out_)))
        else:
            P.c(eng, lambda e: e.tensor_scalar(out=out_, in0=in0, scalar1=s1, scalar2=s2, op0=op0, op1=op1),
                reads, writes, pwrites, cost=ecost(eng, fs(out_)))

    def STT(eng, out_, in0, scalar, in1, op0, op1, reads=(), writes=(), pwrites=()):
        P.c(eng, lambda e: e.scalar_tensor_tensor(out=out_, in0=in0, scalar=scalar, in1=in1, op0=op0, op1=op1),
            reads, writes, pwrites, cost=ecost(eng, fs(out_)))

    PE_GHZ = [1.2]

    def MM(out_, lhsT, rhs, start, stop, reads, writes=(), pwrites=()):
        P.c("pe", lambda e: e.matmul(out_, lhsT, rhs, start=start, stop=stop), reads, writes, pwrites,
            cost=max(110.0 * 1.2 / PE_GHZ[0], fs(out_) / PE_GHZ[0]))

    def TR(out_, in_, ident, reads, writes=(), pwrites=()):
        P.c("pe", lambda e: e.transpose(out=out_, in_=in_, identity=ident), reads, writes, pwrites, cost=110.0)

    def RECIP(out_, in_, reads, writes):
        P.c("dve", lambda e: e.reciprocal(out=out_, in_=in_), reads, writes, cost=(150.0 + 6.5 * fs(out_)) / 0.96)

    with ExitStack() as top:
        def sb(es, name, shape, dt):
            return es.enter_context(nc.sbuf_tensor(name, list(shape), dt))

        def ps(es, name, shape, dt=F32):
            return es.enter_context(nc.psum_tensor(name, list(shape), dt))

        identf = sb(top, "identf", [128, 128], F32)
        identb = sb(top, "identb", [128, 128], BF16)
        iotab = sb(top, "iotab", [128, 128], BF16)
        g2_sb = sb(top, "g2_sb", [128, 8], F32)

        for k_ in ("identf", "identb", "iotab", "g2_sb"):
            P.alias[k_] = "C"
        DMA("sp", identf[:], cident[:], "c", pwrites=["C"])
        DMA("pool", identb[:], cident[:], "c", pwrites=["C"])
        DMA("pool", iotab[:], ciota[:], "c", pwrites=["C"])
        DMA("sp", g2_sb[:], g2_l[:], "c", pwrites=["C"])

        with ExitStack() as es:
            w_in_sb = sb(es, "w_in_sb", [128, 8, 1408], BF16)
            woa_sb = sb(es, "woa_sb", [64, 8, 1024], BF16)
            wob_sb = sb(es, "wob_sb", [128, 4, 1024], BF16)
            wq_sb = sb(es, "wq_sb", [128, 8, 2048], BF16)
            skT_sb = sb(es, "skT_sb", [128, 16, 128], BF16)
            poolw_sb = sb(es, "poolw_sb", [128, 4, 128], BF16)
            bias_sb = sb(es, "bias_sb", [128, 3, 8, 128], F32)
            pinv_sb = sb(es, "pinv_sb", [128, 4, 16], F32)
            cblk_sb = sb(es, "cblk_sb", [128, 128], BF16)
            i16_sb = sb(es, "i16_sb", [128, 2, 16], F32)
            pscale_sb = sb(es, "pscale_sb", [128, 4], F32)
            g1_sb = sb(es, "g1_sb", [128, 8], F32)
            qg_sb = sb(es, "qg_sb", [128, 1], F32)
            kg_sb = sb(es, "kg_sb", [128, 1], F32)
            esink = sb(es, "esink", [64, 8], F32)
            onesb = sb(es, "onesb", [128, 64], BF16)
            epsq = sb(es, "epsq", [128, 1], F32)

            DMA("pool", w_in_sb[:], w_in_l[:], "w", pwrites=["W"])
            DMA("pool", woa_sb[:], woa_l[:], "w", pwrites=["W"])
            DMA("pool", wob_sb[:], wob_l[:], "w", pwrites=["W"])
            DMA("pool", wq_sb[:], wq_l[:], "w", pwrites=["W"])
            DMA("pool", skT_sb[:], skT_l[:], "w", pwrites=["W"])
            DMA("pool", poolw_sb[:], poolw_l[:], "w", pwrites=["W"])
            DMA("sp", bias_sb[:], biasT[:], "w", pwrites=["W"])
            DMA("sp", pinv_sb[:], pool_inv[:], "w", pwrites=["W"])
            DMA("pool", cblk_sb[:], cblk[:], "w", pwrites=["W"])
            DMA("sp", i16_sb[:], ci16[:], "w", pwrites=["W"])
            DMA("sp", pscale_sb[:], pscale_l[:], "w", pwrites=["W"])
            DMA("sp", g1_sb[:], g1_l[:], "w", pwrites=["W"])
            DMA("sp", qg_sb[:], qg_l[:], "w", pwrites=["W"])
            DMA("sp", kg_sb[:], kg_l[:], "w", pwrites=["W"])
            DMA("sp", esink[:], sink_l[:], "w", pwrites=["W"])
            for k_ in ("w_in_sb", "woa_sb", "wob_sb", "wq_sb", "skT_sb", "poolw_sb", "bias_sb", "pinv_sb",
                       "cblk_sb", "i16_sb", "pscale_sb", "g1_sb", "qg_sb"):
                P.alias[k_] = "W"
            ACTF(esink[:], esink[:], AF.Exp, reads=["W"], writes=["esink"])
            TS("dve", kg_sb[:], kg_sb[:], 8.0, None, ALU.mult, None, reads=["W"], writes=["kg_sb"])
            P.c("pool", lambda e: e.memset(onesb[:], 1.0), writes=["onesb"])
            P.c("pool", lambda e: e.memset(epsq[:], 64.0 * EPS), writes=["epsq"])
            epsd = sb(es, "epsd", [128, 1], F32)
            P.c("pool", lambda e: e.memset(epsd[:], EPS), writes=["epsd"])

            xt = [sb(es, f"xt{i}", [128, D], F32) for i in range(2)]
            ssq = [sb(es, f"ssq{i}", [128, 1], F32) for i in range(2)]
            hb = sb(es, "hb", [128, D], BF16)
            hT = sb(es, "hT", [128, 8, 128], BF16)
            sq = sb(es, "sq", [128, 6, 128], BF16)
            rs = sb(es, "rs", [128, 6, 128], F32)
            qnT = sb(es, "qnT", [128, 4, 128], BF16)
            knT = [sb(es, f"knT{i}", [128, 2, 128], BF16) for i in range(2)]
            vtok = [sb(es, f"vtok{i}", [128, 128], BF16) for i in range(2)]
            sfull = sb(es, "sfull", [128, 2, 4, 128], F32)
            PT = sb(es, "PT", [128, 2, 4, 128], BF16)
            rden = sb(es, "rden", [64, 512], F32)
            aT = sb(es, "aT", [64, 8, 128], BF16)
            pbuf = [sb(es, f"pbuf{i}", [128, 4, 144], F32) for i in range(2)]
            sA = sb(es, "sA", [128, 4, 144], F32)
            sB = sb(es, "sB", [128, 4, 144], F32)
            ptmp = sb(es, "ptmp", [128, 4, 16], F32)
            dT = sb(es, "dT", [128, 4, 128], BF16)
            bT = sb(es, "bT", [128, 4, 128], BF16)
            x1 = [sb(es, "x1_0", [128, D], F32)] * 2
            h2b = sb(es, "h2b", [128, D], BF16)
            h2T = [sb(es, f"h2T{i}", [128, 8, 128], BF16) for i in range(2)]
            qpT = sb(es, "qpT", [128, 16, 128], BF16)
            scd = [sb(es, f"sc{i}", [128, 16, 128], F32) for i in range(2)]
            rst = [sb(es, f"rst{i}", [128, 3, 128], BF16) for i in range(2)]
            v16 = sb(es, "v16", [128, 16, 16], F32)
            i16ud = [sb(es, f"i16u{i}", [128, 16, 16], U32) for i in range(2)]
            i16f = sb(es, "i16f", [128, 16, 16], F32)
            cand = sb(es, "cand", [128, 8, 256], F32)
            bestd = [sb(es, f"best{i}", [128, 8, 16], F32) for i in range(2)]
            posud = [sb(es, f"posu{i}", [128, 8, 16], U32) for i in range(2)]
            posf = sb(es, "posf", [128, 8, 16], F32)
            r2f = sb(es, "r2f", [128, 8, 16], F32)
            r1f = sb(es, "r1f", [128, 8, 16], F32)
            eq = sb(es, "eq", [128, 8, 16, 16], BF16)
            ej = sb(es, "ej", [128, 3, 128], F32)
            eb = sb(es, "eb", [128, 8, 16], F32)
            zs = sb(es, "zs", [128, 8], F32)

            tp = ps(es, "tp", [128, 8, 128], BF16)
            B = [None] + [ps(es, f"bk{i}", [128, 512], F32) if i != 6 else None for i in range(1, 8)]
            tpz = ps(es, "tpz", [128, 8, 128], BF16)
            ust = [sb(es, f"ust{i}", [128, D], BF16) for i in range(3)]
            utb = [sb(es, f"utb{i}", [128, 8, 128], BF16) for i in range(2)]

            def bank(i):
                return B[i]

            def front(bi):
                b = bi - 1
                s2 = bi % 2
                o2 = 1 - s2
                X = ("xt", s2)
                DMA("sp", xt[s2][:], xh[bi * 128:(bi + 1) * 128, :], f"xld{s2}", writes=[X])
                ACTF(hb[:], xt[s2][:], AF.Square, reads=[X], writes=["hb", ("ssq", 0)],
                     accum_out=ssq[0][:])
                ACTF(ssq[0][:], ssq[0][:], AF.Ln, reads=[("ssq", 0), "epsd"], writes=[("ssq", 0)],
                     scale=1.0 / D, bias=epsd[:, 0:1])
                ACTF(ssq[0][:], ssq[0][:], AF.Exp, reads=[("ssq", 0)], writes=[("ssq", 0)], scale=-0.5)
                P.c("act", lambda e: e.mul(out=hb[:], in_=xt[s2][:], mul=ssq[0][:, 0:1]),
                    reads=[X, ("ssq", 0)], writes=["hb"])
                for dc in range(8):
                    TR(tp[:, dc, :], hb[:, dc * 128:(dc + 1) * 128], identb[:],
                       reads=["hb", "identb"], pwrites=["tp"])
                TTOP("dve", hT[:], tp[:], g1_sb[:].unsqueeze(2).to_broadcast([128, 8, 128]), ALU.mult,
                     reads=["tp", "g1_sb"], writes=["hT"])
                if a_stage < 1:
                    return
                yq, ykv, yp = bank(1), bank(2), bank(3)
                if b >= 0:
                    for c in range(4):
                        for dc in range(8):
                            MM(yq[:, c * 128:(c + 1) * 128], w_in_sb[:, dc, c * 128:(c + 1) * 128],
                               hT[:, dc, :], dc == 0, dc == 7, reads=["hT", "w_in_sb"], pwrites=[("bk", 1)])
                for c in range(2):
                    for dc in range(8):
                        MM(ykv[:, c * 128:(c + 1) * 128], w_in_sb[:, dc, 512 + c * 128:512 + (c + 1) * 128],
                           hT[:, dc, :], dc == 0, dc == 7, reads=["hT", "w_in_sb"], pwrites=[("bk", 2)])
                for dc in range(8):
                    MM(ykv[:, 256:384], hT[:, dc, :], w_in_sb[:, dc, 768:896],
                       dc == 0, dc == 7, reads=["hT", "w_in_sb"], pwrites=[("bk", 2)])
                for c in range(4):
                    for dc in range(8):
                        MM(yp[:, c * 128:(c + 1) * 128], w_in_sb[:, dc, 896 + c * 128:896 + (c + 1) * 128],
                           hT[:, dc, :], dc == 0, dc == 7, reads=["hT", "w_in_sb"], pwrites=[("bk", 3)])
                if a_stage < 2:
                    return
                sqf = sq[:].rearrange("p a b -> p (a b)")
                rsf = rs[:].rearrange("p a b -> p (a b)")
                ssqb, sskb = bank(4), bank(4)
                if b >= 0:
                    ACTF(sqf[:, 0:512], yq[:], AF.Square, reads=[("bk", 1)], writes=["sq_q"])
                    MM(ssqb[:], cblk_sb[:], sqf[:, 0:512], True, True, reads=["sq_q", "cblk_sb"],
                       writes=[("bk", 4)])
                    ACTF(rsf[:, 0:512], ssqb[:], AF.Ln, reads=[("bk", 4), "epsq"], writes=["rs_q"],
                         bias=epsq[:, 0:1])
                    ACTF(rsf[:, 0:512], rsf[:, 0:512], AF.Exp, reads=["rs_q"], writes=["rs_q"], scale=-0.5)
                    STT("dve", qnT[:].rearrange("p a b -> p (a b)"), yq[:], qg_sb[:, 0:1], rsf[:, 0:512],
                        ALU.mult, ALU.mult, reads=[("bk", 1), "rs_q", "qg_sb"], writes=["qnT"])
                ACTF(sqf[:, 512:768], ykv[:, 0:256], AF.Square, reads=[("bk", 2)], writes=["sq_k"])
                MM(sskb[:, 0:256], cblk_sb[:], sqf[:, 512:768], True, True, reads=["sq_k", "cblk_sb"],
                   writes=[("bk", 4)])
                ACTF(rsf[:, 512:768], sskb[:, 0:256], AF.Ln, reads=[("bk", 4), "epsq"], writes=["rs_k"],
                     bias=epsq[:, 0:1])
                ACTF(rsf[:, 512:768], rsf[:, 512:768], AF.Exp, reads=["rs_k"], writes=["rs_k"], scale=-0.5)
                KN = ("knT", s2)
                STT("dve", knT[s2][:].rearrange("p a b -> p (a b)"), ykv[:, 0:256], kg_sb[:, 0:1],
                    rsf[:, 512:768], ALU.mult, ALU.mult, reads=[("bk", 2), "rs_k", "kg_sb"], writes=[KN])
                VT = ("vtok", s2)
                COPY("act", vtok[s2][:], ykv[:, 256:384], reads=[("bk", 2)], writes=[VT])
                PB = ("pbuf", s2)
                COPY("act", pbuf[s2][:, :, 16:144], yp[:].rearrange("p (a b) -> p a b", a=4),
                     reads=[("bk", 3)], pwrites=[PB])
                if bi == 0:
                    return
                COPY("pool", pbuf[s2][:, :, 0:16], pbuf[o2][:, :, 128:144],
                     reads=[("pbuf", o2)], pwrites=[PB])
                if a_stage < 3:
                    return
                SL = [0, 2, 1, 3]
                for g in range(2):
                    bpar = [bank(3), bank(2)]
                    bpk = [3, 2]
                    for kb in range(2):
                        ks = o2 if kb == 0 else s2
                        for s_ in (0, 2, 1, 3):
                            hh = SL[s_]
                            h = g * 4 + hh
                            r0 = (h % 2) * 64
                            par = s_ // 2
                            col = kb * 256 + (s_ % 2) * 128
                            MM(bpar[par][:, col:col + 128], knT[ks][r0:r0 + 64, g, :],
                               qnT[r0:r0 + 64, h // 2, :], True, True,
                               reads=[("knT", ks), "qnT"], pwrites=[("bk", bpk[par])])
                    for par in range(2):
                        bv = bpar[par][:].rearrange("p (k s q) -> p k s q", k=2, s=2)
                        hs0 = g * 4 + par * 2
                        if bi != 1:
                            TTOP("dve", sfull[:, :, par * 2:par * 2 + 2, :], bv,
                                 bias_sb[:, 0:2, hs0:hs0 + 2, :], ALU.add,
                                 reads=[("bk", bpk[par]), "bias_sb"], pwrites=["sfull"])
                        else:
                            for kb in range(2):
                                TTOP("dve", sfull[:, kb, par * 2:par * 2 + 2, :], bv[:, kb],
                                     bias_sb[:, 2 if kb == 0 else 1, hs0:hs0 + 2, :], ALU.add,
                                     reads=[("bk", bpk[par]), "bias_sb"], pwrites=["sfull"])
                    ACTF(PT[:].rearrange("p k s q -> p (k s q)"), sfull[:].rearrange("p k s q -> p (k s q)"),
                         AF.Exp, reads=["sfull"], writes=["PT"])
                    num, den = bank(4), bank(1)
                    for kb in range(2):
                        ks = o2 if kb == 0 else s2
                        MM(num[0:64, :], vtok[ks][:, g * 64:(g + 1) * 64],
                           PT[:, kb].rearrange("p s q -> p (s q)"), kb == 0, kb == 1,
                           reads=[("vtok", ks), "PT"], pwrites=[("bk", 4)])
                    for kb in range(2):
                        MM(den[0:64, :], onesb[:, 0:64], PT[:, kb].rearrange("p s q -> p (s q)"),
                           kb == 0, kb == 1, reads=["onesb", "PT"], pwrites=[("bk", 1)])
                    for s_ in range(4):
                        h = g * 4 + SL[s_]
                        ACTF(rden[:, s_ * 128:(s_ + 1) * 128], den[0:64, s_ * 128:(s_ + 1) * 128], AF.Ln,
                             reads=[("bk", 1), "esink"], pwrites=["rden"], bias=esink[:, h:h + 1])
                    ACTF(rden[:], rden[:], AF.Exp, reads=["rden"], writes=["rden"], scale=-1.0)
                    TTOP("dve", aT[:, g * 4:(g + 1) * 4, :].rearrange("p a b -> p (a b)"), num[0:64, :],
                         rden[:], ALU.mult, reads=[("bk", 4), "rden"], pwrites=["aT"])
                if a_stage < 4:
                    return
                pb = pbuf[s2]
                TTOP("pool", sA[:, :, 1:144], pb[:, :, 1:144], pb[:, :, 0:143], ALU.add,
                     reads=[PB], writes=["sA"])
                WIN = [2.0, 4.0, 8.0, 16.0]

                def pool_out(g, S, key):
                    if bi == 1:
                        TTOP("pool", ptmp[:, g, :], S[:, g, 16:32], pinv_sb[:, g, :], ALU.mult,
                             reads=[key, "pinv_sb"], pwrites=["ptmp"])
                    TS("pool", S[:, g, 16:144], S[:, g, 16:144], 1.0 / WIN[g], None, ALU.mult, None,
                       reads=[key, "ptmp"], writes=[key])
                    TTOP("pool", dT[:, g, :], S[:, g, 16:144], pb[:, g, 16:144], ALU.subtract,
                         reads=[key, PB], pwrites=["dT"])
                    if bi == 1:
                        TTOP("pool", dT[:, g, 0:16], ptmp[:, g, :], pb[:, g, 16:32], ALU.subtract,
                             reads=["ptmp", PB, "dT"], pwrites=["dT"])
                pool_out(0, sA, "sA")
                TTOP("pool", sB[:, 1:4, 3:144], sA[:, 1:4, 3:144], sA[:, 1:4, 1:142], ALU.add,
                     reads=["sA"], writes=["sB"])
                pool_out(1, sB, "sB")
                TTOP("pool", sA[:, 2:4, 7:144], sB[:, 2:4, 7:144], sB[:, 2:4, 3:140], ALU.add,
                     reads=["sB"], writes=["sA"])
                pool_out(2, sA, "sA")
                TTOP("pool", sB[:, 3:4, 15:144], sA[:, 3:4, 15:144], sA[:, 3:4, 7:136], ALU.add,
                     reads=["sA"], writes=["sB"])
                pool_out(3, sB, "sB")
                po = bank(3)
                for g in range(4):
                    MM(po[:, g * 128:(g + 1) * 128], poolw_sb[:, g, :], dT[:, g, :], True, True,
                       reads=["poolw_sb", "dT"], pwrites=[("bk", 3)])
                TTOP("dve", bT[:], po[:].rearrange("p (a b) -> p a b", a=4),
                     pscale_sb[:].unsqueeze(2).to_broadcast([128, 4, 128]), ALU.mult,
                     reads=[("bk", 3), "pscale_sb"], writes=["bT"])
                if a_stage < 5:
                    return
                X1 = ("x1", 0)
                for half in range(2):
                    wo = bank(1 + half)
                    cs = slice(half * 512, (half + 1) * 512)
                    for h in range(8):
                        MM(wo[:], aT[:, h, :], woa_sb[:, h, cs], h == 0, False,
                           reads=["aT", "woa_sb"], pwrites=[("bk", 1 + half)])
                    for g in range(4):
                        MM(wo[:], bT[:, g, :], wob_sb[:, g, cs], False, g == 3,
                           reads=["bT", "wob_sb"], pwrites=[("bk", 1 + half)])
                    TTOP("dve", x1[s2][:, cs], wo[:], xt[s2][:, cs], ALU.add,
                         reads=[("bk", 1 + half), X], pwrites=[X1])
                DMA("sp", x1s[b * 128:(b + 1) * 128, :], x1[s2][:], "x1st", reads=[X1],
                    pwrites=["x1s"])
                if a_stage < 6:
                    return
                ACTF(h2b[:], x1[s2][:], AF.Square, reads=[X1], writes=["h2b", ("ssq", 1)],
                     accum_out=ssq[1][:])
                ACTF(ssq[1][:], ssq[1][:], AF.Ln, reads=[("ssq", 1), "epsd"], writes=[("ssq", 1)],
                     scale=1.0 / D, bias=epsd[:, 0:1])
                ACTF(ssq[1][:], ssq[1][:], AF.Exp, reads=[("ssq", 1)], writes=[("ssq", 1)], scale=-0.5)
                P.c("act", lambda e: e.mul(out=h2b[:], in_=x1[s2][:], mul=ssq[1][:, 0:1]),
                    reads=[X1, ("ssq", 1)], writes=["h2b"])
                for dc in range(8):
                    TR(tp[:, dc, :], h2b[:, dc * 128:(dc + 1) * 128], identb[:],
                       reads=["h2b", "identb"], pwrites=["tp"])
                H2 = ("h2T", s2)
                TTOP("dve", h2T[s2][:], tp[:], g2_sb[:].unsqueeze(2).to_broadcast([128, 8, 128]), ALU.mult,
                     reads=["tp", "g2_sb"], writes=[H2])
                DMA("sp", h2s[b // 2][:, :, (b % 2) * 128:(b % 2 + 1) * 128], h2T[s2][:], "h2st",
                    reads=[H2], pwrites=["h2s"])

            def front3(bi):
                b = bi - 1
                s2 = bi % 2
                H2 = ("h2T", s2)
                if a_stage < 7:
                    return
                scs = bi % 2
                sc = scd[scs]
                qbanks = [7, 7, 7, 7]
                for grp in range(4):
                    qb = bank(qbanks[grp])
                    for cc in range(4):
                        c = grp * 4 + cc
                        for dc in range(8):
                            MM(qb[:, cc * 128:(cc + 1) * 128], wq_sb[:, dc, c * 128:(c + 1) * 128],
                               h2T[s2][:, dc, :], dc == 0, dc == 7, reads=[H2, "wq_sb"],
                               pwrites=[("bk", qbanks[grp])])
                    COPY("act", qpT[:, grp * 4:(grp + 1) * 4, :].rearrange("p a b -> p (a b)"), qb[:],
                         reads=[("bk", qbanks[grp])], pwrites=["qpT"])
                sbanks = [7, 7, 7, 7]
                for grp in range(4):
                    sbk = bank(sbanks[grp])
                    for cc in range(4):
                        c = grp * 4 + cc
                        MM(sbk[:, cc * 128:(cc + 1) * 128], qpT[:, c, :], skT_sb[:, c, :], True, True,
                           reads=["qpT", "skT_sb"], pwrites=[("bk", sbanks[grp])])
                    COPY("act", sc[:, grp * 4:(grp + 1) * 4, :].rearrange("p a b -> p (a b)"), sbk[:],
                         reads=[("bk", sbanks[grp])], writes=[("sc", scs, grp * 4 + cc) for cc in range(4)])

            def route1(bi):
                b = bi - 1
                scs = bi % 2
                sc = scd[scs]
                best, posu, i16u = bestd[scs], posud[scs], i16ud[scs]
                if a_stage < 8:
                    return
                for c in range(16):
                    P.c("dve", (lambda c=c: lambda e: e.max(out=v16[:, c, 0:8], in_=sc[:, c, :]))(),
                        reads=[("sc", scs, c)], writes=[("v16a", c)], cost=200)
                for c in range(16):
                    P.c("dve", (lambda c=c: lambda e: e.max_index(
                        out=i16u[:, c, 0:8], in_max=v16[:, c, 0:8], in_values=sc[:, c, :]))(),
                        reads=[("sc", scs, c), ("v16a", c)], pwrites=[("i16u", scs)], cost=270)
                for c in range(16):
                    P.c("dve", (lambda c=c: lambda e: e.match_replace(
                        out=sc[:, c, :], in_to_replace=v16[:, c, 0:8], in_values=sc[:, c, :], imm_value=-1e30))(),
                        reads=[("sc", scs, c), ("v16a", c)], writes=[("sc", scs, c)], cost=270)
                for c in range(16):
                    P.c("dve", (lambda c=c: lambda e: e.max(out=v16[:, c, 8:16], in_=sc[:, c, :]))(),
                        reads=[("sc", scs, c)], writes=[("v16b", c)], cost=200)
                for c in range(16):
                    P.c("dve", (lambda c=c: lambda e: e.max_index(
                        out=i16u[:, c, 8:16], in_max=v16[:, c, 8:16], in_values=sc[:, c, :]))(),
                        reads=[("sc", scs, c), ("v16b", c)], pwrites=[("i16u", scs)], cost=270)
                v4 = v16[:].rearrange("p (h q) r -> p h q r", q=2)
                TTOP("dve", cand[:].rearrange("p h (a b) -> p h a b", a=16),
                     v4[:, :, 0, :].unsqueeze(3).to_broadcast([128, 8, 16, 16]),
                     v4[:, :, 1, :].unsqueeze(2).to_broadcast([128, 8, 16, 16]), ALU.add,
                     reads=[("v16a", c) for c in range(16)] + [("v16b", c) for c in range(16)],
                     writes=[("cand", h) for h in range(8)])
                for h in range(8):
                    P.c("dve", (lambda h=h: lambda e: e.max(out=best[:, h, 0:8], in_=cand[:, h, :]))(),
                        reads=[("cand", h)], writes=[("besta", scs, h)], cost=300)
                for h in range(8):
                    P.c("dve", (lambda h=h: lambda e: e.max_index(
                        out=posu[:, h, 0:8], in_max=best[:, h, 0:8], in_values=cand[:, h, :]))(),
                        reads=[("cand", h), ("besta", scs, h)], pwrites=[("posu", scs)], cost=330)
                for h in range(8):
                    P.c("dve", (lambda h=h: lambda e: e.match_replace(
                        out=cand[:, h, :], in_to_replace=best[:, h, 0:8], in_values=cand[:, h, :],
                        imm_value=-1e30))(), reads=[("cand", h), ("besta", scs, h)], writes=[("cand", h)], cost=330)
                for h in range(8):
                    P.c("dve", (lambda h=h: lambda e: e.max(out=best[:, h, 8:16], in_=cand[:, h, :]))(),
                        reads=[("cand", h)], writes=[("bestb", scs, h)], cost=300)
                for h in range(8):
                    P.c("dve", (lambda h=h: lambda e: e.max_index(
                        out=posu[:, h, 8:16], in_max=best[:, h, 8:16], in_values=cand[:, h, :]))(),
                        reads=[("cand", h), ("bestb", scs, h)], pwrites=[("posu", scs)], cost=330)

            def route2(bi):
                b = bi - 1
                scs = bi % 2
                best, posu, i16u = bestd[scs], posud[scs], i16ud[scs]
                if a_stage < 9:
                    return
                TTOP("dve", eb[:], best[:], best[:, :, 0:1].to_broadcast([128, 8, 16]), ALU.subtract,
                     reads=[("besta", scs, h) for h in range(8)] + [("bestb", scs, h) for h in range(8)], writes=["eb"])
                ACTF(eb[:], eb[:], AF.Exp, reads=["eb"], writes=["eb"])
                P.c("dve", lambda e: e.tensor_reduce(out=zs[:], in_=eb[:], axis=AX.X, op=ALU.add),
                    reads=["eb"], writes=["zs"])
                RECIP(zs[:], zs[:], reads=["zs"], writes=["zs"])
                TTOP("dve", ej[:, 2, :].rearrange("p (h k) -> p h k", h=8), eb[:],
                     zs[:].unsqueeze(2).to_broadcast([128, 8, 16]), ALU.mult,
                     reads=["eb", "zs"], pwrites=["ej"])
                if a_stage < 10:
                    return
                COPY("pool", i16f[:], i16u[:], reads=[("i16u", scs)], writes=["i16f"])
                P.c("dve", lambda e: e.tensor_single_scalar(out=posf[:].bitcast(U32), in_=posu[:], scalar=4,
                                                            op=ALU.logical_shift_right),
                    reads=[("posu", scs)], writes=["posf"])
                COPY("pool", r1f[:], posf[:].bitcast(U32), reads=["posf"], writes=["r1f"])
                P.c("dve", lambda e: e.tensor_single_scalar(out=posf[:].bitcast(U32), in_=posu[:], scalar=15,
                                                            op=ALU.bitwise_and),
                    reads=[("posu", scs), "r1f"], writes=["posf"])
                COPY("pool", r2f[:], posf[:].bitcast(U32), reads=["posf"], writes=["r2f"])
                i4 = i16f[:].rearrange("p (h q) r -> p h q r", q=2)
                for which, rf, key in ((0, r1f, "r1f"), (1, r2f, "r2f")):
                    TTOP("dve", eq[:], rf[:].unsqueeze(3).to_broadcast([128, 8, 16, 16]),
                         i16_sb[:, 1, :].unsqueeze(1).unsqueeze(1).to_broadcast([128, 8, 16, 16]),
                         ALU.is_equal, reads=[key, "i16_sb"], writes=["eq"])
                    TTOP("dve", eq[:], eq[:], i4[:, :, which, :].unsqueeze(2).to_broadcast([128, 8, 16, 16]),
                         ALU.mult, reads=["eq", "i16f"], writes=["eq"])
                    P.c("dve", (lambda which=which: lambda e: e.tensor_reduce(
                        out=ej[:, which, :].rearrange("p (h k) -> p h k", h=8), in_=eq[:], axis=AX.X,
                        op=ALU.add))(), reads=["eq"], pwrites=["ej"], cost=2290)
                if a_stage < 11:
                    return
                rb = bank(5)
                for w3 in range(3):
                    TR(rb[:, w3 * 128:(w3 + 1) * 128], ej[:, w3, :], identf[:], reads=["ej", "identf"],
                       pwrites=[("bk", 5)])
                COPY("act", rst[scs][:], rb[:, 0:384].rearrange("p (a b) -> p a b", a=3),
                     reads=[("bk", 5)], writes=[("rst", scs)])
                DMA("sp", Rs[b // 2][:, :, (b % 2) * 128:(b % 2 + 1) * 128], rst[scs][:], f"rst{scs}",
                    reads=[("rst", scs)], pwrites=["Rs"])

            def captured(fn, *a_):
                P.capture = []
                fn(*a_)
                ops_ = P.capture
                P.capture = None
                return ops_

            def zconv(bi):
                if only_a or bi < 1:
                    return
                for e1 in range((bi - 1) * 4, bi * 4):
                    s_ = e1 % 3
                    t_ = e1 % 2
                    DMA("pool", ust[s_][:], peer_u[e1 * 128:(e1 + 1) * 128, :], f"uld{s_}",
                        writes=[("ust", s_)])
                    for dc in range(8):
                        TR(tpz[:, dc, :], ust[s_][:, dc * 128:(dc + 1) * 128], identb[:],
                           reads=[("ust", s_), "identb"], pwrites=["tpz"])
                    COPY("act", utb[t_][:], tpz[:], reads=["tpz"], writes=[("utb", t_)])
                    DMA("sp", UTs[e1], utb[t_][:].rearrange("p a b -> p (a b)"), f"ust{t_}",
                        reads=[("utb", t_)], pwrites=["UTs"])

            if not only_a:
                for i in range(64):
                    DMA("pool", Vs[i * 256:(i + 1) * 256, :], peer_v[i * 256:(i + 1) * 256, :],
                        "vc", pwrites=["Vs"])
            F3p, T1p, T2p = [], [], []
            for bi in range(nblk_run + 1):
                Fops = captured(front, bi)
                Zops = captured(zconv, bi)
                P.schedule(Fops, F3p, T1p, T2p, Zops)
                T2p = captured(route2, bi - 2) if bi >= 3 else []
                T1p = captured(route1, bi - 1) if bi >= 2 else []
                F3p = captured(front3, bi) if bi >= 1 else []
            nb_ = nblk_run
            P.schedule(F3p, T1p, T2p)
            if nb_ >= 1:
                P.schedule(captured(route1, nb_), captured(route2, nb_ - 1) if nb_ >= 2 else [])
                P.schedule(captured(route2, nb_))
            P.flush()
            if stop_after == "A":
                return nc

        with ExitStack() as es:
            Gt = [sb(es, f"Gt{i}", [128, 128, TT], BF16) for i in range(2)]
            A_oh = [sb(es, f"A_oh{i}", [128, 4, 128], BF16) for i in range(2)]
            B_oh = [sb(es, f"B_oh{i}", [128, 4, 128], BF16) for i in range(2)]
            NS = 4
            ubuf = [sb(es, f"ubuf{i}", [128, 8, 128], BF16) for i in range(NS)]
            vbuf = [sb(es, f"vbuf{i}", [128, D], BF16) for i in range(NS)]
            h2t = [sb(es, f"h2t{i}", [128, 8, TT], BF16) for i in range(2)]
            gl = [sb(es, f"gl{i}", [128, TT], BF16) for i in range(2)]
            GA = [sb(es, f"GA{i}", [128, TT], BF16) for i in range(2)]
            x1t = sb(es, "x1t", [128, 2, D], F32)
            ot = sb(es, "ot", [128, 2, D], F32)
            acc = [[ps(es, f"acc{a}{d}", [128, 512], F32) for d in range(2)] for a in range(2)]
            actp = [ps(es, f"actp{i}", [128, 512], F32) for i in range(2)]
            gps = [ps(es, f"gps{i}", [128, 4, 128], F32) for i in range(2)]

            nsteps = NTILE * 128

            def load_step(sidx):
                if sidx >= nsteps:
                    return
                e1 = sidx % 128
                sl = sidx % NS
                DMA("sp", ubuf[sl][:].rearrange("p a b -> p (a b)"), UTs[e1], f"ul{sl}",
                    reads=["UTs"], writes=[("ubuf", sl)])
                DMA("sp", vbuf[sl][:], Vs[e1 * 128:(e1 + 1) * 128, :], f"vl{sl}",
                    reads=["Vs"], writes=[("vbuf", sl)])

            Rt = [sb(es, f"Rt{i}", [128, 3, TT], BF16) for i in range(2)]

            def load_rt(tile):
                if tile >= NTILE:
                    return
                DMA("sp", Rt[tile % 2][:], Rs[tile], f"rtl{tile % 2}", reads=["Rs"], writes=[("Rt", tile % 2)])

            def g_oh(tile, tl):
                if tile >= NTILE:
                    return
                rt = Rt[tile % 2]
                q4, k = tl // 4, tl % 4
                sl = q4 % 2
                TS("dve", A_oh[sl][:, k, :], iotab[:], rt[:, 0, tl:tl + 1], rt[:, 2, tl:tl + 1],
                   ALU.is_equal, ALU.mult, reads=["iotab", ("Rt", tile % 2)], writes=[("A_oh", sl, k)])
                TS("dve", B_oh[sl][:, k, :], iotab[:], rt[:, 1, tl:tl + 1], None,
                   ALU.is_equal, None, reads=["iotab", ("Rt", tile % 2)], writes=[("B_oh", sl, k)])

            def g_mm(tile, tl):
                if tile >= NTILE:
                    return
                q4, k = tl // 4, tl % 4
                sl = q4 % 2
                MM(gps[sl][:, k, :], B_oh[sl][:, k, :], A_oh[sl][:, k, :], True, True,
                   reads=[("A_oh", sl, k), ("B_oh", sl, k)], pwrites=[("gps", sl)])

            def g_ev(tile, q4):
                if tile >= NTILE:
                    return
                sl = q4 % 2
                t4 = q4 * 4
                COPY("act", Gt[tile % 2][:, :, t4:t4 + 4], gps[sl][:].rearrange("p t e -> p e t"),
                     reads=[("gps", sl)], pwrites=[("Gt", tile % 2)])

            for pre in range(NS - 1):
                load_step(pre)
            load_rt(0)
            for tl in range(TT):
                g_oh(0, tl)
                g_mm(0, tl)
                if tl % 4 == 3:
                    g_ev(0, tl // 4)

            for tau in range(NTILE):
                tok0 = tau * TT
                hs = tau % 2
                gb = tau % 2
                DMA("sp", h2t[hs][:], h2s[tau], f"h2l{hs}", reads=["h2s"], writes=[("h2t", hs)])
                load_rt(tau + 1)

                def u_stage(e1):
                    sidx = tau * 128 + e1
                    sl = sidx % NS
                    a = e1 % 2
                    for dc in range(8):
                        MM(actp[a][:, 0:TT], ubuf[sl][:, dc, :], h2t[hs][:, dc, :], dc == 0, dc == 7,
                           reads=[("ubuf", sl), ("h2t", hs)], pwrites=[("actp", a)])
                    ACTF(gl[a][:], actp[a][:, 0:TT], AF.Gelu, reads=[("actp", a)], writes=[("gl", a)])
                    TTOP("dve", GA[a][:], gl[a][:], Gt[gb][:, e1, :], ALU.mult,
                         reads=[("gl", a), ("Gt", gb)], writes=[("GA", a)])

                def v_stage(e1):
                    sidx = tau * 128 + e1
                    sl = sidx % NS
                    a = e1 % 2
                    for ts_ in range(2):
                        for dh in range(2):
                            MM(acc[ts_][dh][:], GA[a][:, ts_ * 128:(ts_ + 1) * 128],
                               vbuf[sl][:, dh * 512:(dh + 1) * 512], e1 == 0, e1 == 127,
                               reads=[("GA", a), ("vbuf", sl)], pwrites=[("acc", ts_, dh)])

                for e1 in range(128):
                    u_stage(e1)
                    if e1 >= 1:
                        v_stage(e1 - 1)
                    load_step(tau * 128 + e1 + NS - 1)
                    g_oh(tau + 1, 2 * e1)
                    g_oh(tau + 1, 2 * e1 + 1)
                    if e1 >= 1:
                        g_mm(tau + 1, 2 * e1 - 2)
                        g_mm(tau + 1, 2 * e1 - 1)
                    if e1 >= 3 and e1 % 2 == 1:
                        g_ev(tau + 1, (e1 - 3) // 2)
                v_stage(127)
                g_mm(tau + 1, 254)
                g_mm(tau + 1, 255)
                g_ev(tau + 1, 63)
                DMA("sp", x1t[:], x1s[tok0:tok0 + TT, :].rearrange("(a p) d -> p a d", p=128),
                    "x1l", reads=["x1s"], writes=["x1t"])
                for ts_ in range(2):
                    for dh in range(2):
                        TTOP("dve", ot[:, ts_, dh * 512:(dh + 1) * 512], acc[ts_][dh][:],
                             x1t[:, ts_, dh * 512:(dh + 1) * 512], ALU.add,
                             reads=[("acc", ts_, dh), "x1t"], pwrites=["ot"])
                DMA("sp", out[tok0:tok0 + TT, :].rearrange("(a p) d -> p a d", p=128), ot[:], "ost",
                    reads=["ot"], pwrites=["out"])
            P.flush()
    return nc


def _t5_bucket(rel):
    nb = 16
    max_exact = 8
    base = np.where(rel > 0, nb, 0)
    n = np.abs(rel)
    nf = np.maximum(n, 1).astype(np.float32)
    large = max_exact + (np.log(nf / np.float32(max_exact)) / np.float32(np.log(128 / max_exact))
                         * np.float32(nb - max_exact)).astype(np.int32)
    large = np.minimum(large, nb - 1)
    return base + np.where(n < max_exact, n, large)


HORD = [0, 2, 1, 3, 4, 6, 5, 7]


def _bias_tables(rel_bias, seq_start):
    ki = np.arange(128)[:, None]
    qi = np.arange(128)[None, :]
    cq = qi // 64
    out = np.empty((128, 3, 8, 128), np.float32)
    for var, off in ((0, -128), (1, 0)):
        rel = (off + ki) - qi
        bk = _t5_bucket(rel)
        ckr = ki // 64 + (off // 64)
        valid = (ckr >= cq - 2) & (ckr <= cq)
        vals = rel_bias[bk]
        vals = np.where(valid[:, :, None], vals, np.float32(NEG))
        out[:, var] = np.transpose(vals, (0, 2, 1))[:, HORD, :]
    out[:, 2] = np.float32(NEG) if seq_start else out[:, 0]
    return out


_NC_CACHE = {}


def kernel(x, norm1_g, w_in, q_norm_g, k_norm_g, attn_sinks, rel_bias, pool_w, pool_scale,
           w_out, norm2_g, peer_wq, peer_subkeys, peer_u, peer_v, _debug=False, _stop=None, _dbgA=None):
    f = np.float32
    x = np.asarray(x, f)
    w_in0 = np.asarray(w_in, f)[0]
    wext = np.concatenate([w_in0[:, 0:512], w_in0[:, 512:576], w_in0[:, 512:576],
                           w_in0[:, 576:640], w_in0[:, 576:640], w_in0[:, 640:768],
                           w_in0[:, 768:1280]], axis=1)

    def pmaj(w):
        return np.ascontiguousarray(w.reshape(8, 128, -1).transpose(1, 0, 2))

    w_out0 = np.asarray(w_out, f)[0]
    shared = {
        "cident": np.eye(128, dtype=f),
        "ciota": np.tile(np.arange(128, dtype=f), (128, 1)),
        "cblk": np.kron(np.eye(2, dtype=f), np.ones((64, 64), f)),
        "ci16": np.tile(np.stack([16.0 * np.arange(16), np.arange(16)]).astype(f)[None], (128, 1, 1)),
        "w_in_l": pmaj(wext),
        "woa_l": np.ascontiguousarray(w_out0[0:512].reshape(8, 64, 1024)[HORD].transpose(1, 0, 2)),
        "wob_l": np.ascontiguousarray(w_out0[512:].reshape(4, 128, 1024).transpose(1, 0, 2)),
        "wq_l": pmaj(np.asarray(peer_wq, f)[0]),
        "skT_l": np.ascontiguousarray(np.asarray(peer_subkeys, f)[0].reshape(16, 128, 128).transpose(2, 0, 1)),
        "poolw_l": np.ascontiguousarray(np.asarray(pool_w, f)[0].transpose(1, 0, 2)),
        "pscale_l": np.ascontiguousarray(np.asarray(pool_scale, f)[0].reshape(4, 128).T),
        "g1_l": np.ascontiguousarray(np.asarray(norm1_g, f)[0].reshape(8, 128).T),
        "g2_l": np.ascontiguousarray(np.asarray(norm2_g, f)[0].reshape(8, 128).T),
        "qg_l": np.tile(np.asarray(q_norm_g, f)[0], 2).reshape(128, 1),
        "kg_l": np.tile(np.asarray(k_norm_g, f)[0], 2).reshape(128, 1),
        "sink_l": np.tile(np.asarray(attn_sinks, f)[0][None, :], (64, 1)),
        "peer_u": np.asarray(peer_u, f)[0],
        "peer_v": np.asarray(peer_v, f)[0],
    }
    rb = np.asarray(rel_bias, f)
    bias_tabs = {True: _bias_tables(rb, True), False: _bias_tables(rb, False)}
    in_maps = []
    for c in range(NCORES):
        bidx, half = c // 2, c % 2
        start = half * TOK
        xh = np.zeros((TOK + 128, D), f)
        xh[128:] = x[bidx, start:start + TOK]
        if half == 1:
            xh[:128] = x[bidx, start - 128:start]
        pinv = np.empty((128, 4, 16), f)
        for g, w in enumerate((2, 4, 8, 16)):
            t = np.arange(16)
            pinv[:, g, :] = (1.0 / np.minimum(t + 1, w) if half == 0 else np.full(16, 1.0 / w))
        m = dict(shared)
        m["xh"] = xh
        m["biasT"] = bias_tabs[half == 0]
        m["pool_inv"] = pinv
        in_maps.append(m)
    if _dbgA is not None:
        nc = build(debug=True, stop_after="A", only_a=True, nblk_run=_dbgA[0], a_stage=_dbgA[1])
        for m in in_maps:
            del m["peer_u"], m["peer_v"]
        res = run_bass_kernel_spmd(nc, in_maps, core_ids=list(range(NCORES)))
        return None, res
    key = (bool(_debug), _stop)
    if key not in _NC_CACHE:
        _NC_CACHE[key] = build(debug=bool(_debug), stop_after=_stop)
    nc = _NC_CACHE[key]
    res = run_bass_kernel_spmd(nc, in_maps, core_ids=list(range(NCORES)))
    outs = [np.asarray(r["out"]) for r in res.results]
    full = np.stack(outs, 0).reshape(4, 2 * TOK, D).astype(np.float32)
    if _debug:
        return full, res
    return full
```
